# Optimizing a Trainium2 kernel written in Bass

```python
import math
import jax, jax.numpy as jnp
from jax import lax
import numpy as np

D_MODEL = 1024
BATCH = 1
SEQ = 16384
DEPTH = 1
DEC_BATCH = 32
DEC_SEQ = 64
PAST_LEN = 1024

CHUNK = 64
Q_BLOCK = 128
D_A = 64
H_A = D_MODEL // 128
H_I = 8
D_I = 32
TOPK_MAX = 256
H_B = D_MODEL // 256
D_BQK = 32
D_BV = 64
H_C = D_MODEL // 256
D_C = 64
N_MEM = 256
N_BUCKETS = 32
MAX_DIST = 128
EPS = 1e-6
NEG = -1e30
W_A = H_A * D_A
W_B = H_B * D_BV
W_C = H_C * D_C
D_MIX = W_A + W_B + W_C
SPLIT_SIZES = (W_A, W_A, W_A, W_A, H_I * D_I, D_I, H_I,
               H_B * 2 * D_BQK, H_B * 2 * D_BQK, W_B, W_B, W_C, W_C)
SPLIT_POINTS = tuple(int(v) for v in np.cumsum(SPLIT_SIZES)[:-1])
D_IN = int(sum(SPLIT_SIZES))

kernel_name = "hybrid_dsa_diff_mem_streaming_step"


def rmsnorm(x, g):
    xf = x.astype(jnp.float32)
    y = xf * lax.rsqrt(jnp.mean(xf * xf, axis=-1, keepdims=True) + EPS)
    return (y * g.astype(jnp.float32)).astype(x.dtype)


def rel_bucket(rel):
    nb = N_BUCKETS // 2
    ret = jnp.where(rel > 0, nb, 0)
    n = jnp.abs(rel)
    max_exact = nb // 2
    nf = jnp.maximum(n, 1).astype(jnp.float32)
    large = max_exact + (jnp.log(nf / max_exact) / math.log(MAX_DIST / max_exact)
                         * (nb - max_exact)).astype(jnp.int32)
    large = jnp.minimum(large, nb - 1)
    return ret + jnp.where(n < max_exact, n, large)


def qkv_proj(x, ln_g, w_in, a_qn, a_kn, b_qn, b_kn, c_qn):
    bn, t, _ = x.shape
    h = rmsnorm(x, ln_g)
    aq, ak, av, ag, iq, ik, iw, bq, bk, bv, bg, cq, cg = jnp.split(h @ w_in, SPLIT_POINTS, axis=-1)
    aq = rmsnorm(aq.reshape(bn, t, H_A, D_A), a_qn)
    ak = rmsnorm(ak.reshape(bn, t, H_A, D_A), a_kn)
    av = av.reshape(bn, t, H_A, D_A)
    iq = iq.reshape(bn, t, H_I, D_I)
    bq = rmsnorm(bq.reshape(bn, t, H_B, 2, D_BQK), b_qn)
    bk = rmsnorm(bk.reshape(bn, t, H_B, 2, D_BQK), b_kn)
    bv = bv.reshape(bn, t, H_B, D_BV)
    cq = rmsnorm(cq.reshape(bn, t, H_C, D_C), c_qn)
    return aq, ak, av, ag, iq, ik, iw, bq, bk, bv, bg, cq, cg


def mem_kv(mem, mem_ln, w_mem_kv, c_kn):
    bn, m, _ = mem.shape
    mk, mv = jnp.split(rmsnorm(mem, mem_ln) @ w_mem_kv, 2, axis=-1)
    return rmsnorm(mk.reshape(bn, m, H_C, D_C), c_kn), mv.reshape(bn, m, H_C, D_C)


def dsa_attend(aq, iq, iw, ak, av, ik, q_pos, k_pos, tab, topk):
    tq = q_pos.shape[0]
    adm = (k_pos[None, :] // CHUNK) <= (q_pos[:, None] // CHUNK)
    dots = jnp.einsum('bthd,bsd->bths', iq, ik)
    score = jnp.einsum('bth,bths->bts', iw, jax.nn.relu(dots)).astype(jnp.float32)
    score = jnp.where(adm[None], score, NEG)
    _, idx = lax.top_k(score, topk)
    valid = adm[jnp.arange(tq)[None, :, None], idx]
    k_sel = jax.vmap(lambda kb, ib: kb[ib])(ak, idx)
    v_sel = jax.vmap(lambda vb, ib: vb[ib])(av, idx)
    logits = jnp.einsum('bthd,btkhd->bthk', aq, k_sel).astype(jnp.float32) * (D_A ** -0.5)
    rel = k_pos[idx] - q_pos[None, :, None]
    logits = logits + jnp.swapaxes(tab[rel_bucket(rel)], -1, -2).astype(jnp.float32)
    logits = jnp.where(valid[:, :, None, :], logits, NEG)
    p = jax.nn.softmax(logits, axis=-1).astype(av.dtype)
    return jnp.einsum('bthk,btkhd->bthd', p, v_sel)


def diff_attend(bq, bk, bv, q_pos, k_pos, tab, lam, lam_init, subln_g):
    s = jnp.einsum('bthcd,bshcd->bchts', bq, bk).astype(jnp.float32) * (D_BQK ** -0.5)
    rel = k_pos[None, :] - q_pos[:, None]
    bias = jnp.transpose(tab[rel_bucket(rel)], (2, 0, 1)).astype(jnp.float32)
    adm = (k_pos[None, :] // CHUNK) <= (q_pos[:, None] // CHUNK)
    p = jax.nn.softmax(jnp.where(adm, s + bias, NEG), axis=-1)
    attn = p[:, 0] - lam * p[:, 1]
    o = jnp.einsum('bhts,bshd->bthd', attn.astype(bv.dtype), bv)
    return rmsnorm(o, subln_g) * (1.0 - lam_init)


def mem_attend(cq, mk, mv):
    s = jnp.einsum('bthd,bmhd->bhtm', cq, mk).astype(jnp.float32) * (D_C ** -0.5)
    p = jax.nn.softmax(s, axis=-1).astype(mv.dtype)
    return jnp.einsum('bhtm,bmhd->bthd', p, mv)


def diff_lambda(lq1, lk1, lq2, lk2, lam_init):
    f32 = jnp.float32
    return (jnp.exp(jnp.sum(lq1.astype(f32) * lk1.astype(f32)))
            - jnp.exp(jnp.sum(lq2.astype(f32) * lk2.astype(f32))) + lam_init)


def merge(x, oa, ob, oc, ag, bg, cg, w_out):
    bn, t, _ = x.shape
    o = jnp.concatenate([oa.reshape(bn, t, W_A) * jax.nn.silu(ag),
                         ob.reshape(bn, t, W_B) * jax.nn.silu(bg),
                         oc.reshape(bn, t, W_C) * jax.nn.silu(cg)], axis=-1)
    return x + o @ w_out


def setup_inputs(seed: int = 0) -> dict:
    key = jax.random.key(seed)
    ks = jax.random.split(key, 32)
    f32 = jnp.float32

    def nrm(k, shape, s=1.0):
        return jax.random.normal(k, shape, f32) * s

    def gain(k, shape):
        return 1.0 + 0.02 * jax.random.normal(k, shape, f32)

    return {
        "x_prompt": nrm(ks[0], (BATCH, SEQ, D_MODEL)),
        "x_sample": nrm(ks[1], (DEC_BATCH, DEC_SEQ, D_MODEL)),
        "mem_prompt": nrm(ks[2], (BATCH, N_MEM, D_MODEL)),
        "cache_a_k": nrm(ks[3], (DEPTH, DEC_BATCH, PAST_LEN, H_A, D_A)),
        "cache_a_v": nrm(ks[4], (DEPTH, DEC_BATCH, PAST_LEN, H_A, D_A)),
        "cache_a_kidx": nrm(ks[5], (DEPTH, DEC_BATCH, PAST_LEN, D_I)),
        "cache_b_k": nrm(ks[6], (DEPTH, DEC_BATCH, PAST_LEN, H_B, 2, D_BQK)),
        "cache_b_v": nrm(ks[7], (DEPTH, DEC_BATCH, PAST_LEN, H_B, D_BV)),
        "cache_mem_k": nrm(ks[8], (DEPTH, DEC_BATCH, N_MEM, H_C, D_C)),
        "cache_mem_v": nrm(ks[9], (DEPTH, DEC_BATCH, N_MEM, H_C, D_C)),
        "rel_table": nrm(ks[10], (N_BUCKETS, H_A + H_B), 0.5),
        "ln_g": gain(ks[11], (DEPTH, D_MODEL)),
        "w_in": nrm(ks[12], (DEPTH, D_MODEL, D_IN), D_MODEL ** -0.5),
        "w_out": nrm(ks[13], (DEPTH, D_MIX, D_MODEL), D_MIX ** -0.5),
        "a_qn": gain(ks[14], (DEPTH, D_A)),
        "a_kn": gain(ks[15], (DEPTH, D_A)),
        "b_qn": gain(ks[16], (DEPTH, D_BQK)),
        "b_kn": gain(ks[17], (DEPTH, D_BQK)),
        "b_subln": gain(ks[18], (DEPTH, D_BV)),
        "lam_q1": nrm(ks[19], (DEPTH, D_BQK), 0.1),
        "lam_k1": nrm(ks[20], (DEPTH, D_BQK), 0.1),
        "lam_q2": nrm(ks[21], (DEPTH, D_BQK), 0.1),
        "lam_k2": nrm(ks[22], (DEPTH, D_BQK), 0.1),
        "c_qn": gain(ks[23], (DEPTH, D_C)),
        "c_kn": gain(ks[24], (DEPTH, D_C)),
        "mem_ln": gain(ks[25], (DEPTH, D_MODEL)),
        "w_mem_kv": nrm(ks[26], (DEPTH, D_MODEL, 2 * W_C), D_MODEL ** -0.5),
    }


def reference(x_prompt, x_sample, mem_prompt, cache_a_k, cache_a_v, cache_a_kidx, cache_b_k,
              cache_b_v, cache_mem_k, cache_mem_v, rel_table, ln_g, w_in, w_out, a_qn, a_kn,
              b_qn, b_kn, b_subln, lam_q1, lam_k1, lam_q2, lam_k2, c_qn, c_kn, mem_ln, w_mem_kv):
    tab_a = rel_table[:, :H_A]
    tab_b = rel_table[:, H_A:]

    t_p = x_prompt.shape[1]
    pos_p = jnp.arange(t_p, dtype=jnp.int32)
    topk_p = min(TOPK_MAX, t_p // 4)
    n_blk = t_p // Q_BLOCK

    def to_blocks(t):
        return jnp.moveaxis(t.reshape((t.shape[0], n_blk, Q_BLOCK) + t.shape[2:]), 1, 0)

    def from_blocks(t):
        t = jnp.moveaxis(t, 0, 1)
        return t.reshape((t.shape[0], t_p) + t.shape[3:])

    p_ak, p_av, p_aki, p_bk, p_bv, p_mk, p_mv = [], [], [], [], [], [], []
    h = x_prompt
    for l in range(DEPTH):
        lam_init = 0.8 - 0.6 * math.exp(-0.3 * l)
        lam = diff_lambda(lam_q1[l], lam_k1[l], lam_q2[l], lam_k2[l], lam_init)
        aq, ak, av, ag, iq, ik, iw, bq, bk, bv, bg, cq, cg = qkv_proj(
            h, ln_g[l], w_in[l], a_qn[l], a_kn[l], b_qn[l], b_kn[l], c_qn[l])
        mk, mv = mem_kv(mem_prompt, mem_ln[l], w_mem_kv[l], c_kn[l])
        sub_g = b_subln[l]

        def block_step(args):
            aq_b, iq_b, iw_b, bq_b, qp = args
            oa_b = dsa_attend(aq_b, iq_b, iw_b, ak, av, ik, qp, pos_p, tab_a, topk_p)
            ob_b = diff_attend(bq_b, bk, bv, qp, pos_p, tab_b, lam, lam_init, sub_g)
            return oa_b, ob_b

        oa, ob = lax.map(block_step, (to_blocks(aq), to_blocks(iq), to_blocks(iw), to_blocks(bq),
                                      pos_p.reshape(n_blk, Q_BLOCK)))
        oc = mem_attend(cq, mk, mv)
        h = merge(h, from_blocks(oa), from_blocks(ob), oc, ag, bg, cg, w_out[l])
        p_ak.append(ak); p_av.append(av); p_aki.append(ik)
        p_bk.append(bk); p_bv.append(bv); p_mk.append(mk); p_mv.append(mv)
    y_prompt = h

    t_s = x_sample.shape[1]
    past = cache_a_k.shape[2]
    l_tot = past + t_s
    q_pos_s = past + jnp.arange(t_s, dtype=jnp.int32)
    k_pos_s = jnp.arange(l_tot, dtype=jnp.int32)
    topk_s = min(TOPK_MAX, l_tot // 4)

    s_ak, s_av, s_aki, s_bk, s_bv = [], [], [], [], []
    g = x_sample
    for l in range(DEPTH):
        lam_init = 0.8 - 0.6 * math.exp(-0.3 * l)
        lam = diff_lambda(lam_q1[l], lam_k1[l], lam_q2[l], lam_k2[l], lam_init)
        aq, ak, av, ag, iq, ik, iw, bq, bk, bv, bg, cq, cg = qkv_proj(
            g, ln_g[l], w_in[l], a_qn[l], a_kn[l], b_qn[l], b_kn[l], c_qn[l])
        akf = jnp.concatenate([cache_a_k[l], ak], axis=1)
        avf = jnp.concatenate([cache_a_v[l], av], axis=1)
        ikf = jnp.concatenate([cache_a_kidx[l], ik], axis=1)
        bkf = jnp.concatenate([cache_b_k[l], bk], axis=1)
        bvf = jnp.concatenate([cache_b_v[l], bv], axis=1)

        def dsa_one(args):
            q1, i1, w1, k1, v1, ik1 = args
            return dsa_attend(q1[None], i1[None], w1[None], k1[None], v1[None], ik1[None],
                              q_pos_s, k_pos_s, tab_a, topk_s)[0]

        oa = lax.map(dsa_one, (aq, iq, iw, akf, avf, ikf))
        ob = diff_attend(bq, bkf, bvf, q_pos_s, k_pos_s, tab_b, lam, lam_init, b_subln[l])
        oc = mem_attend(cq, cache_mem_k[l], cache_mem_v[l])
        g = merge(g, oa, ob, oc, ag, bg, cg, w_out[l])
        s_ak.append(ak); s_av.append(av); s_aki.append(ik); s_bk.append(bk); s_bv.append(bv)
    y_sample = g

    p_a_k = jnp.stack(p_ak); p_a_v = jnp.stack(p_av); p_a_kidx = jnp.stack(p_aki)
    p_b_k = jnp.stack(p_bk); p_b_v = jnp.stack(p_bv)
    p_mem_k = jnp.stack(p_mk); p_mem_v = jnp.stack(p_mv)
    s_a_k = jnp.stack(s_ak); s_a_v = jnp.stack(s_av); s_a_kidx = jnp.stack(s_aki)
    s_b_k = jnp.stack(s_bk); s_b_v = jnp.stack(s_bv)
    return (y_prompt, y_sample, p_a_k, p_a_v, p_a_kidx, p_b_k, p_b_v, p_mem_k, p_mem_v,
            s_a_k, s_a_v, s_a_kidx, s_b_k, s_b_v)
```

```python
import bisect as _bisect
import contextlib
import math

import numpy as np
import ml_dtypes

import concourse.bass as bass
import concourse.mybir as mybir
from concourse.bass_utils import run_bass_kernel_spmd

F32 = mybir.dt.float32
BF16 = mybir.dt.bfloat16
ALU = mybir.AluOpType
AF = mybir.ActivationFunctionType
AX = mybir.AxisListType

NCORE = 8
D = 1024
SEQ = 16384
NSLOT = 16
NSAMP = 4
NJOB = NSLOT + NSAMP
P_ROWS = 16384
S_ROWS = 1536
NK = P_ROWS + NSAMP * S_ROWS
EPS = 1e-6
NEGM = -30000.0
NITER = 18
DIN = 3880
C_AQ, C_AK, C_AV, C_AG, C_IQ, C_IK, C_IW, C_BQ, C_BK, C_BV, C_BG, C_CQ, C_CG = (
    0, 512, 1024, 1536, 2048, 2304, 2336, 2344, 2600, 2856, 3112, 3368, 3624)
FULL_GROUPS = [(0, 512), (512, 1024), (1024, 1536), (1536, 2048), (2048, 2344), (2344, 2856),
               (2856, 3368), (3368, 3880)]
K_GROUPS = [(512, 1024), (1024, 1536), (2304, 2336), (2600, 2856), (2856, 3112)]
G_AQ, G_AK, G_BQ, G_BK, G_CQ, G_CK, G_SUB = 0, 512, 1024, 1280, 1536, 1792, 2048
NGAIN = 2304


class _Op:
    __slots__ = ("eng", "fn", "reads", "writes", "dma", "semkey", "deps", "sig", "val", "sem")

    def __init__(self, eng, fn, reads, writes, dma, semkey):
        self.eng, self.fn, self.reads, self.writes, self.dma, self.semkey = eng, fn, reads, writes, dma, semkey
        self.deps, self.sig, self.val, self.sem = (), False, 0, None


class Sched:
    ENGS = ("pe", "act", "dve", "pool", "sp")

    def __init__(self, nc, sem_chunk=30000):
        self.nc, self.ops, self.sem_chunk = nc, [], sem_chunk
        self.barriers = []

    def barrier(self):
        self.barriers.append(len(self.ops))

    def add(self, eng, fn, reads=(), writes=(), dma=False, semkey=None):
        if dma and semkey is None:
            semkey = ("dma", (tuple(writes) + tuple(reads))[0])
        op = _Op(eng, fn, tuple(reads), tuple(writes), dma, semkey)
        self.ops.append(op)
        return op

    def pe(self, fn, reads=(), writes=()):
        return self.add("pe", fn, reads, writes)

    def act(self, fn, reads=(), writes=()):
        return self.add("act", fn, reads, writes)

    def dve(self, fn, reads=(), writes=()):
        return self.add("dve", fn, reads, writes)

    def pool(self, fn, reads=(), writes=()):
        return self.add("pool", fn, reads, writes)

    def dma(self, out, in_, reads=(), writes=(), semkey=None, q="sp"):
        return self.add(q, lambda e: e.dma_start(out=out, in_=in_), reads, writes, dma=True, semkey=semkey)

    def analyze(self):
        last_w, readers, ops = {}, {}, self.ops
        for i, op in enumerate(ops):
            deps = set()
            for k in op.reads:
                w = last_w.get(k)
                if w is not None:
                    deps.add(w)
            for k in op.writes:
                w = last_w.get(k)
                if w is not None:
                    deps.add(w)
                deps.update(readers.get(k, ()))
            deps.discard(i)
            if op.eng == "pe":
                deps = {d for d in deps if ops[d].eng != "pe"}
            op.deps = deps
            for d in deps:
                ops[d].sig = True
            for k in op.reads:
                readers.setdefault(k, []).append(i)
            for k in op.writes:
                last_w[k] = i
                readers[k] = []
        for m in self.barriers:
            last_eng, last_dma = {}, {}
            for i in range(m):
                if ops[i].dma:
                    last_dma[ops[i].semkey] = i
                else:
                    last_eng[ops[i].eng] = i
            seen = set()
            for i in range(m, len(ops)):
                if ops[i].eng in seen:
                    continue
                seen.add(ops[i].eng)
                extra = set(last_dma.values())
                for en, li in last_eng.items():
                    if not (en == "pe" and ops[i].eng == "pe"):
                        extra.add(li)
                ops[i].deps = set(ops[i].deps) | extra
                for d in extra:
                    ops[d].sig = True
                if len(seen) == len(self.ENGS):
                    break
        for op in ops:
            if op.dma:
                op.sig = True

    def emit(self):
        nc, ops = self.nc, self.ops
        self.analyze()
        eng_cnt = {e: 0 for e in self.ENGS}
        dma_cnt, sem_names = {}, {}
        for op in ops:
            if not op.sig:
                continue
            if op.dma:
                c = dma_cnt.get(op.semkey, 0) + 1
                dma_cnt[op.semkey] = c
                op.sem, op.val = ("d", op.semkey), 16 * c
            else:
                c = eng_cnt[op.eng]
                op.sem, op.val = ("e", op.eng, c // self.sem_chunk), c % self.sem_chunk + 1
                eng_cnt[op.eng] = c + 1
            sem_names[op.sem] = None
        stack = contextlib.ExitStack()
        sems = {k: stack.enter_context(nc.semaphore("s%d" % n)) for n, k in enumerate(sem_names)}
        self.n_sems = len(sems)
        dma_upto = {}
        for i, op in enumerate(ops):
            if op.dma:
                dma_upto.setdefault(op.semkey, []).append((i, op.val))
        by_eng = {e: [] for e in self.ENGS}
        for i, op in enumerate(ops):
            by_eng[op.eng].append(i)
        final_waits = {("d", k): lst[-1][1] for k, lst in dma_upto.items()}

        def run(engname, eng):
            waited = {}
            maxchunk = {}
            for i in by_eng[engname]:
                op = ops[i]
                need = {}
                for d in op.deps:
                    dop = ops[d]
                    if dop.dma:
                        lst = dma_upto[dop.semkey]
                        pos = _bisect.bisect_left(lst, (i, -1)) - 1
                        sem, val = dop.sem, lst[pos][1]
                    else:
                        sem, val = dop.sem, dop.val
                    if need.get(sem, 0) < val:
                        need[sem] = val
                for sem, val in need.items():
                    if waited.get(sem, 0) >= val:
                        continue
                    if sem[0] == "e" and maxchunk.get(sem[1], -1) > sem[2]:
                        continue
                    eng.wait_ge(sems[sem], val)
                    waited[sem] = val
                    if sem[0] == "e":
                        maxchunk[sem[1]] = max(maxchunk.get(sem[1], -1), sem[2])
                ins = op.fn(eng)
                if op.sig:
                    ins.then_inc(sems[op.sem], 16 if op.dma else 1)
            if engname == "sp":
                for sem, val in final_waits.items():
                    eng.wait_ge(sems[sem], val)

        with nc.Block() as block:
            @block.tensor
            def _(e):
                run("pe", e)

            @block.scalar
            def _(e):
                run("act", e)

            @block.vector
            def _(e):
                run("dve", e)

            @block.gpsimd
            def _(e):
                run("pool", e)

            @block.sync
            def _(e):
                run("sp", e)
        stack.close()


TM_NAMES = ["aq", "ak", "av", "ag", "iq", "ik", "iw", "bq", "bk", "bv", "bg", "cq", "cg"]
FULL_GROUPS = [((0, 512), ["aq"]), ((512, 1024), ["ak"]), ((1024, 1536), ["av"]), ((1536, 2048), ["ag"]),
               ((2048, 2344), ["iq", "ik", "iw"]), ((2344, 2856), ["bq", "bk"]), ((2856, 3368), ["bv", "bg"]),
               ((3368, 3880), ["cq", "cg"])]
K_GROUPS = [((512, 1024), ["ak"]), ((1024, 1536), ["av"]), ((2304, 2336), ["ik"]), ((2600, 2856), ["bk"]),
            ((2856, 3112), ["bv"])]
MEM_GROUPS = [((0, 512), ["aq"])]


def build_nc(nslot=NSLOT, nsamp=NSAMP, do_attn=True, niter=NITER, debug=False):
    nc = bass.Bass("TRN2", target_bir_lowering=False)
    S = Sched(nc)
    njob = NJOB

    def din(name, shape, dt=F32):
        return nc.dram_tensor(name, list(shape), dt, kind="ExternalInput").ap()

    def dout(name, shape, dt=F32):
        return nc.dram_tensor(name, list(shape), dt, kind="ExternalOutput").ap()

    def dscr(name, shape, dt):
        return nc.dram_tensor(name, list(shape), dt, kind="Internal").ap()

    xT_all = din("xT_all", [D, P_ROWS])
    xqT = din("xqT", [D, njob * 128])
    xq = din("xq", [njob * 128, D])
    memT = din("memT", [D, 256])
    w_in = din("w_in", [D, DIN])
    w_out = din("w_out", [D, D])
    w_mem = din("w_mem", [D, 512])
    lng = din("lng", [128, 16])
    gains_d = din("gains", [128, NGAIN])
    lamv_d = din("lamv", [128, 128])
    tabp_d = din("tabp", [128, 128])
    oneh_d = din("oneh", [128, 384])
    cst_d = din("cst", [128, 640])
    fmask_d = din("fmask", [128, 2 * 896])
    valid_d = din("valid", [128, 128])
    c_kT = din("c_kT", [NSAMP, 128, 7, S_ROWS])
    c_v = din("c_v", [NSAMP, S_ROWS, 780])
    c_mkT = din("c_mkT", [NSAMP, 128, 2, 256])
    c_mv = din("c_mv", [NSAMP, 256, 260])
    y_o = dout("y", [njob * 128, D])
    ak_o = dout("o_ak", [njob * 128, 512])
    av_o = dout("o_av", [njob * 128, 512])
    ik_o = dout("o_ik", [njob * 128, 32])
    bk_o = dout("o_bk", [njob * 128, 256])
    bv_o = dout("o_bv", [njob * 128, 256])
    mk_o = dout("o_mk", [256, 256])
    mv_o = dout("o_mv", [256, 256])
    dbg_o = dout("dbg_o", [njob * 128, D]) if debug else None
    kT_s = dscr("kT_s", [128, 7, NK], BF16)
    v_s = dscr("v_s", [NK, 780], BF16)
    mk_s = dscr("mk_s", [5, 128, 2, 256], BF16)
    mv_s = dscr("mv_s", [5, 256, 260], BF16)
    q_s = dscr("q_s", [njob, 128, 28, 128], BF16)
    tok_s = dscr("tok_s", [njob, 128, 1032], F32)
    tvec = dscr("tvec", [12, 384], F32)
    wout_s = dscr("wout_s", [128, 8, 1024], BF16)

    st = contextlib.ExitStack()

    def sb(name, shape, dt, stack=st):
        return stack.enter_context(nc.sbuf_tensor("sb_" + name, list(shape), dt))

    def ps(name, shape, dt, stack=st):
        return stack.enter_context(nc.psum_tensor("ps_" + name, list(shape), dt))

    with st:
        cst = sb("cst", [128, 640], F32)
        ident = sb("ident", [128, 128], BF16)
        ones_bf = sb("ones_bf", [128, 1], BF16)
        gains = sb("gains", [128, NGAIN], F32)
        S.dma(cst[:], cst_d, writes=["cst"])
        S.dma(gains[:], gains_d, writes=["gains"])
        S.dve(lambda e: e.tensor_copy(out=ident[:], in_=cst[:, 0:128]), reads=["cst"], writes=["ident"])
        S.dve(lambda e: e.memset(ones_bf[:], 1.0), writes=["ones_bf"])
        S.dve(lambda e: e.tensor_scalar(out=gains[:, G_SUB:G_SUB + 256], in0=gains[:, G_SUB:G_SUB + 256],
                                        scalar1=0.8, scalar2=None, op0=ALU.mult),
              reads=["gains"], writes=["gains"])

        with contextlib.ExitStack() as st1:
            lng_t = sb("lng_t", [128, 16], F32, st1)
            valid = sb("valid", [128, 128], F32, st1)
            win_bf = sb("win_bf", [128, 8, DIN], BF16, st1)
            xg = sb("xg", [128, 8, 512], F32, st1)
            xsq = sb("xsq", [128, 8, 512], BF16, st1)
            xbf = [sb("xbf%d" % i, [128, 8, 512], BF16, st1) for i in range(2)]
            rr = [sb("rr%d" % i, [128, 8], F32, st1) for i in range(2)]
            tm = [sb("tm%d" % i, [128, DIN], F32, st1) for i in range(2)]
            sq = sb("sq", [128, 512], F32, st1)
            nst = sb("nst", [128, 2, 16], F32, st1)
            kbf = [sb("kbf%d" % i, [128, 7, 128], BF16, st1) for i in range(2)]
            kT_stage = [sb("kTst%d" % i, [128, 7, 512], BF16, st1) for i in range(2)]
            v_stage = [sb("vst%d" % i, [128, 4, 12, 65], BF16, st1) for i in range(2)]
            qbf = sb("qbf", [128, 10, 128], BF16, st1)
            qpad = sb("qpad", [128, 28, 128], BF16, st1)
            tokb = sb("tokb", [128, 1032], F32, st1)
            mkf = sb("mkf", [128, 2, 256], F32, st1)
            mvf = sb("mvf", [128, 2, 260], F32, st1)
            mk_stage = sb("mk_stage", [128, 2, 256], BF16, st1)
            mv_stage = sb("mv_stage", [128, 2, 4, 65], BF16, st1)
            ssq_ps = ps("ssq_ps", [128, 8], F32, st1)
            pj_ps = [ps("pj_ps%d" % i, [128, 512], F32, st1) for i in range(3)]
            tp_ps = [ps("tp_ps%d" % i, [128, 8, 128], BF16, st1) for i in range(2)]
            S.dma(valid[:], valid_d, writes=["valid"])
            S.dma(lng_t[:], lng, writes=["lng"])
            S.pool(lambda e: e.memset(qpad[:], 0.0), writes=["qpad"])

            def tmk(tb, names):
                return ["tm%d_%s" % (tb, n) for n in names]

            WIN_KEYS = ["win%d" % c for c in range(8)]
            cnt = {"grp": 0, "tile": 0, "pj": 0, "tp": 0, "sg": 0}

            def load_weight(src, ncols, lcol):
                for c in range(8):
                    b = c % 2
                    S.dma(tm[b][:, 0:ncols], src[c * 128:(c + 1) * 128, :], writes=tmk(b, TM_NAMES))
                    S.dve(lambda e, c=c, b=b: e.tensor_scalar(out=win_bf[:, c, 0:ncols], in0=tm[b][:, 0:ncols],
                                                             scalar1=lng_t[:, lcol + c:lcol + c + 1], scalar2=None, op0=ALU.mult),
                          reads=tmk(b, TM_NAMES) + ["lng"], writes=["win%d" % c])

            def proj_group(src, ntiles):
                g = cnt["grp"]
                cnt["grp"] += 1
                gb = g % 2
                T = ntiles * 128
                S.dma(xg[:, :, 0:T], src.rearrange("(c p) t -> p c t", p=128), writes=["xg"])
                S.act(lambda e: e.activation(out=xsq[:, :, 0:T], in_=xg[:, :, 0:T], func=AF.Square),
                      reads=["xg"], writes=["xsq"])
                S.act(lambda e: e.activation(out=xbf[gb][:, :, 0:T], in_=xg[:, :, 0:T], func=AF.Copy),
                      reads=["xg"], writes=["xbf%d" % gb])
                for i in range(ntiles):
                    for c in range(8):
                        S.pe(lambda e, i=i, c=c: e.matmul(ssq_ps[:, i:i + 1], lhsT=xsq[:, c, i * 128:(i + 1) * 128],
                                                          rhs=ones_bf[:], start=(c == 0), stop=(c == 7)),
                             reads=["xsq", "ones_bf"], writes=["ssq_ps"])
                S.act(lambda e: e.activation(out=rr[gb][:, 4:4 + ntiles], in_=ssq_ps[:, 0:ntiles], func=AF.Sqrt, bias=EPS,
                                             scale=1.0 / D), reads=["ssq_ps"], writes=["rrs%d" % gb])
                S.dve(lambda e: e.reciprocal(out=rr[gb][:, 0:ntiles], in_=rr[gb][:, 4:4 + ntiles]),
                      reads=["rrs%d" % gb], writes=["rr%d" % gb])
                return gb

            def proj_tile(gb, i, groups, hook=None, direct=None):
                t = cnt["tile"]
                cnt["tile"] += 1
                tb = t % 2
                pbs = []

                def evac(k):
                    (c0, c1), names = groups[k]
                    pb = pbs[k]
                    if direct is not None and names[0] in direct:
                        out_ap, in_fn, wkeys = direct[names[0]]
                        S.act(lambda e: e.activation(out=out_ap, in_=in_fn(pj_ps[pb]), func=AF.Copy, scale=rr[gb][:, i:i + 1]),
                              reads=["pj%d" % pb, "rr%d" % gb], writes=wkeys)
                        return
                    S.act(lambda e: e.activation(out=tm[tb][:, c0:c1], in_=pj_ps[pb][:, 0:c1 - c0],
                                                 func=AF.Copy, scale=rr[gb][:, i:i + 1]),
                          reads=["pj%d" % pb, "rr%d" % gb], writes=tmk(tb, names))

                for gi, ((c0, c1), names) in enumerate(groups):
                    pb = cnt["pj"] % 3
                    cnt["pj"] += 1
                    pbs.append(pb)
                    for c in range(8):
                        S.pe(lambda e, c=c, c0=c0, c1=c1, pb=pb: e.matmul(
                            pj_ps[pb][:, 0:c1 - c0], lhsT=xbf[gb][:, c, i * 128:(i + 1) * 128], rhs=win_bf[:, c, c0:c1],
                            start=(c == 0), stop=(c == 7)),
                            reads=["xbf%d" % gb] + WIN_KEYS, writes=["pj%d" % pb])
                    if gi == min(1, len(groups) - 1) and hook is not None:
                        hook()
                    if gi >= 2:
                        evac(gi - 2)
                for k in range(max(0, len(groups) - 2), len(groups)):
                    evac(k)
                return tb

            def norm_inplace(tb, name, col, H, Dh, gcol):
                key = "tm%d_%s" % (tb, name)
                W = H * Dh
                x2 = tm[tb][:, col:col + W]
                x3 = x2.rearrange("p (h d) -> p h d", h=H)
                S.dve(lambda e: e.tensor_tensor(out=sq[:, 0:W], in0=x2, in1=x2, op=ALU.mult), reads=[key], writes=["sq"])
                S.dve(lambda e: e.tensor_reduce(out=nst[:, 0, 0:H], in_=sq[:, 0:W].rearrange("p (h d) -> p h d", h=H),
                                                axis=AX.X, op=ALU.add), reads=["sq"], writes=["nst0"])
                S.act(lambda e: e.activation(out=nst[:, 1, 0:H], in_=nst[:, 0, 0:H], func=AF.Sqrt, bias=EPS,
                                             scale=1.0 / Dh), reads=["nst0"], writes=["nst1"])
                S.dve(lambda e: e.reciprocal(out=nst[:, 0, 0:H], in_=nst[:, 1, 0:H]), reads=["nst1"], writes=["nst0"])
                S.dve(lambda e: e.tensor_tensor(out=x3, in0=x3, in1=nst[:, 0, 0:H].unsqueeze(2).broadcast_to([128, H, Dh]),
                                                op=ALU.mult), reads=["nst0", key], writes=[key])
                S.dve(lambda e: e.tensor_tensor(out=x2, in0=x2, in1=gains[:, gcol:gcol + W], op=ALU.mult),
                      reads=["gains", key], writes=[key])

            def kside(tb, i, sgb, vcol_ap):
                tmb = tm[tb]
                t = cnt["tp"]
                cnt["tp"] += 1
                kb = t % 2
                kk = "kbf%d" % kb
                S.dve(lambda e: e.tensor_copy(out=kbf[kb][:, 0:4, :], in_=tmb[:, C_AK:C_AK + 512].rearrange("p (a b) -> p a b", a=4)),
                      reads=tmk(tb, ["ak"]), writes=[kk + "a"])
                S.pool(lambda e: e.tensor_copy(out=kbf[kb][:, 4:6, :], in_=tmb[:, C_BK:C_BK + 256].rearrange("p (a b) -> p a b", a=2)),
                       reads=tmk(tb, ["bk"]), writes=[kk + "b"])
                S.pool(lambda e: e.tensor_copy(out=kbf[kb][:, 6, :].rearrange("p (a b) -> p a b", a=4),
                                               in_=tmb[:, C_IK:C_IK + 32].unsqueeze(1).broadcast_to([128, 4, 32])),
                       reads=tmk(tb, ["ik"]), writes=[kk + "i"])
                for j in range(7):
                    S.pe(lambda e, j=j: e.transpose(tp_ps[kb][:, j, :], kbf[kb][:, j, :], ident[:]),
                         reads=[kk + "a", kk + "b", kk + "i", "ident"], writes=["tp%d" % kb])
                S.act(lambda e: e.activation(out=kT_stage[sgb][:, :, i * 128:(i + 1) * 128], in_=tp_ps[kb][:, 0:7, :],
                                             func=AF.Copy), reads=["tp%d" % kb], writes=["kTst%d_%d" % (sgb, i)])
                S.pool(lambda e: e.tensor_copy(out=v_stage[sgb][:, i, 0:8, 0:64],
                                               in_=tmb[:, C_AV:C_AV + 512].rearrange("p (h d) -> p h d", h=8)),
                       reads=tmk(tb, ["av"]), writes=["vst%d_%da" % (sgb, i)])
                S.pool(lambda e: e.tensor_copy(out=v_stage[sgb][:, i, 8:12, 0:64],
                                               in_=tmb[:, C_BV:C_BV + 256].rearrange("p (h d) -> p h d", h=4)),
                       reads=tmk(tb, ["bv"]), writes=["vst%d_%db" % (sgb, i)])
                S.pool(lambda e: e.tensor_copy(out=v_stage[sgb][:, i, :, 64:65], in_=vcol_ap.unsqueeze(1).broadcast_to([128, 12, 1])),
                       reads=["valid", "cst"], writes=["vst%d_%dc" % (sgb, i)])

            def kst_keys(sgb, n):
                return ["kTst%d_%d" % (sgb, i) for i in range(n)]

            def vst_keys(sgb, n):
                return [k for i in range(n) for k in ("vst%d_%da" % (sgb, i), "vst%d_%db" % (sgb, i), "vst%d_%dc" % (sgb, i))]

            load_weight(w_mem, 512, 8)
            gbm = proj_group(memT, 2)
            for i in range(2):
                tb = proj_tile(gbm, i, MEM_GROUPS)
                key = "tm%d_aq" % tb
                norm_inplace(tb, "aq", 0, 4, 64, G_CK)
                S.dma(mk_o[i * 128:(i + 1) * 128, :], tm[tb][:, 0:256], reads=[key], semkey=("st", "tm%d" % tb))
                S.dma(mv_o[i * 128:(i + 1) * 128, :], tm[tb][:, 256:512], reads=[key], semkey=("st", "tm%d" % tb))
                t = cnt["tp"]
                cnt["tp"] += 1
                kb = t % 2
                S.dve(lambda e, tb=tb, kb=kb: e.tensor_copy(out=kbf[kb][:, 0:2, :], in_=tm[tb][:, 0:256].rearrange("p (a b) -> p a b", a=2)),
                      reads=[key], writes=["kbf%da" % kb])
                for j in range(2):
                    S.pe(lambda e, j=j, kb=kb: e.transpose(tp_ps[kb][:, j, :], kbf[kb][:, j, :], ident[:]),
                         reads=["kbf%da" % kb, "ident"], writes=["tp%d" % kb])
                S.act(lambda e, kb=kb, i=i: e.activation(out=mk_stage[:, :, i * 128:(i + 1) * 128], in_=tp_ps[kb][:, 0:2, :],
                                                         func=AF.Copy), reads=["tp%d" % kb], writes=["mk_stage%d" % i])
                S.pool(lambda e, tb=tb, i=i: e.tensor_copy(out=mv_stage[:, i, :, 0:64],
                                                           in_=tm[tb][:, 256:512].rearrange("p (h d) -> p h d", h=4)),
                       reads=[key], writes=["mv_stage%da" % i])
                S.pool(lambda e, i=i: e.memset(mv_stage[:, i, :, 64:65], 1.0), writes=["mv_stage%db" % i])
            MKK = ["mk_stage0", "mk_stage1"]
            MVK = ["mv_stage0a", "mv_stage0b", "mv_stage1a", "mv_stage1b"]
            S.dma(mk_s[0], mk_stage[:], reads=MKK, writes=["mk_s0"], semkey=("st", "mk_stage"))
            S.dma(mv_s[0].rearrange("(i p) f -> p i f", p=128), mv_stage[:].rearrange("p i h d -> p i (h d)"),
                  reads=MVK, writes=["mv_s0"], semkey=("st", "mv_stage"))
            for b in range(nsamp):
                S.dma(mkf[:], c_mkT[b], writes=["mkf"])
                S.dve(lambda e: e.tensor_copy(out=mk_stage[:], in_=mkf[:]), reads=["mkf"], writes=MKK)
                S.dma(mk_s[1 + b], mk_stage[:], reads=MKK, writes=["mk_s%d" % (1 + b)], semkey=("st", "mk_stage"))
                S.dma(mvf[:], c_mv[b].rearrange("(i p) f -> p i f", p=128), writes=["mvf"])
                S.dve(lambda e: e.tensor_copy(out=mv_stage[:].rearrange("p i h d -> p i (h d)"), in_=mvf[:]),
                      reads=["mvf"], writes=MVK)
                S.dma(mv_s[1 + b].rearrange("(i p) f -> p i f", p=128), mv_stage[:].rearrange("p i h d -> p i (h d)"),
                      reads=MVK, writes=["mv_s%d" % (1 + b)], semkey=("st", "mv_stage"))

            load_weight(w_in, DIN, 0)

            for b in range(nsamp):
                base = P_ROWS + b * S_ROWS
                for ch in range(3):
                    sgb = cnt["sg"] % 2
                    cnt["sg"] += 1
                    t = cnt["tile"]
                    cnt["tile"] += 1
                    tb = t % 2
                    S.dma(xg[:, 0:7, :], c_kT[b, :, :, ch * 512:(ch + 1) * 512], writes=["xg"])
                    S.dve(lambda e, sgb=sgb: e.tensor_copy(out=kT_stage[sgb][:], in_=xg[:, 0:7, :]),
                          reads=["xg"], writes=kst_keys(sgb, 4))
                    S.dma(kT_s[:, :, base + ch * 512: base + (ch + 1) * 512], kT_stage[sgb][:],
                          reads=kst_keys(sgb, 4), writes=["kT_s_r%d_%d" % (b, ch)], semkey=("st", "kTst%d" % sgb))
                    tmv = tm[tb][:, 0:3120].rearrange("p (i f) -> p i f", i=4)
                    S.dma(tmv, c_v[b, ch * 512:(ch + 1) * 512, :].rearrange("(i p) f -> p i f", p=128),
                          writes=tmk(tb, TM_NAMES))
                    S.pool(lambda e, sgb=sgb, tmv=tmv: e.tensor_copy(out=v_stage[sgb][:].rearrange("p i h d -> p i (h d)"), in_=tmv),
                           reads=tmk(tb, TM_NAMES), writes=vst_keys(sgb, 4))
                    S.dma(v_s[base + ch * 512: base + (ch + 1) * 512, :].rearrange("(i p) f -> p i f", p=128),
                          v_stage[sgb][:].rearrange("p i h d -> p i (h d)"),
                          reads=vst_keys(sgb, 4), writes=["v_s_r%d_%d" % (b, ch)], semkey=("st", "vst%d" % sgb))

            sq2 = [sb("sq2_%d" % i, [128, 768], F32, st1) for i in range(2)]
            nst2 = [sb("nst2_%d" % i, [128, 2, 16], F32, st1) for i in range(2)]
            qbf2 = [qbf, sb("qbf_1", [128, 10, 128], BF16, st1)]

            def norm_kk_a(tb, par):
                tmb = tm[tb]
                kak, kbk = "tm%d_ak" % tb, "tm%d_bk" % tb
                sqb, nb = sq2[par], nst2[par]
                sk, nk = "sq2_%d" % par, "nst2_%d" % par
                xa = tmb[:, C_AK:C_AK + 512]
                xb = tmb[:, C_BK:C_BK + 256]
                S.dve(lambda e: e.scalar_tensor_tensor(out=sqb[:, 0:512], in0=xa, scalar=1.0 / 64, in1=xa, op0=ALU.mult, op1=ALU.mult),
                      reads=[kak], writes=[sk + "a"])
                S.dve(lambda e: e.scalar_tensor_tensor(out=sqb[:, 512:768], in0=xb, scalar=1.0 / 32, in1=xb, op0=ALU.mult, op1=ALU.mult),
                      reads=[kbk], writes=[sk + "b"])
                S.dve(lambda e: e.tensor_reduce(out=nb[:, 0, 0:8], in_=sqb[:, 0:512].rearrange("p (h d) -> p h d", h=8),
                                                axis=AX.X, op=ALU.add), reads=[sk + "a"], writes=[nk + "a"])
                S.dve(lambda e: e.tensor_reduce(out=nb[:, 0, 8:16], in_=sqb[:, 512:768].rearrange("p (h d) -> p h d", h=8),
                                                axis=AX.X, op=ALU.add), reads=[sk + "b"], writes=[nk + "b"])
                S.act(lambda e: e.activation(out=nb[:, 1, :], in_=nb[:, 0, :], func=AF.Sqrt, bias=EPS, scale=1.0),
                      reads=[nk + "a", nk + "b"], writes=[nk + "s"])

            def norm_kk_b(tb, par):
                tmb = tm[tb]
                kak, kbk = "tm%d_ak" % tb, "tm%d_bk" % tb
                nb = nst2[par]
                nk = "nst2_%d" % par
                xa = tmb[:, C_AK:C_AK + 512]
                xb = tmb[:, C_BK:C_BK + 256]
                S.dve(lambda e: e.reciprocal(out=nb[:, 0, :], in_=nb[:, 1, :]), reads=[nk + "s"], writes=[nk + "a", nk + "b"])
                S.dve(lambda e: e.tensor_tensor(out=xa.rearrange("p (h d) -> p h d", h=8), in0=xa.rearrange("p (h d) -> p h d", h=8),
                                                in1=nb[:, 0, 0:8].unsqueeze(2).broadcast_to([128, 8, 64]), op=ALU.mult),
                      reads=[nk + "a", kak], writes=[kak])
                S.dve(lambda e: e.tensor_tensor(out=xb.rearrange("p (h d) -> p h d", h=8), in0=xb.rearrange("p (h d) -> p h d", h=8),
                                                in1=nb[:, 0, 8:16].unsqueeze(2).broadcast_to([128, 8, 32]), op=ALU.mult),
                      reads=[nk + "b", kbk], writes=[kbk])
                S.dve(lambda e: e.tensor_tensor(out=xa, in0=xa, in1=gains[:, G_AK:G_AK + 512], op=ALU.mult),
                      reads=["gains", kak], writes=[kak])
                S.dve(lambda e: e.tensor_tensor(out=xb, in0=xb, in1=gains[:, G_BK:G_BK + 256], op=ALU.mult),
                      reads=["gains", kbk], writes=[kbk])

            ik4b = [sb("ik4b%d" % i, [128, 128], BF16, st1) for i in range(4)]

            def kside1(tb, i, sgb, vcol_ap, par, direct=False, ikr=0):
                tmb = tm[tb]
                kk = "kbf%d" % par
                S.dve(lambda e: e.tensor_copy(out=kbf[par][:, 0:4, :], in_=tmb[:, C_AK:C_AK + 512].rearrange("p (a b) -> p a b", a=4)),
                      reads=tmk(tb, ["ak"]), writes=[kk + "a"])
                S.pool(lambda e: e.tensor_copy(out=kbf[par][:, 4:6, :], in_=tmb[:, C_BK:C_BK + 256].rearrange("p (a b) -> p a b", a=2)),
                       reads=tmk(tb, ["bk"]), writes=[kk + "b"])
                if not direct:
                    S.pool(lambda e: e.tensor_copy(out=ik4b[ikr][:].rearrange("p (a b) -> p a b", a=4),
                                                   in_=tmb[:, C_IK:C_IK + 32].unsqueeze(1).broadcast_to([128, 4, 32])),
                           reads=tmk(tb, ["ik"]), writes=["ik4b%d" % ikr])
                    S.pool(lambda e: e.tensor_copy(out=v_stage[sgb][:, i, 0:8, 0:64],
                                                   in_=tmb[:, C_AV:C_AV + 512].rearrange("p (h d) -> p h d", h=8)),
                           reads=tmk(tb, ["av"]), writes=["vst%d_%da" % (sgb, i)])
                    S.pool(lambda e: e.tensor_copy(out=v_stage[sgb][:, i, 8:12, 0:64],
                                                   in_=tmb[:, C_BV:C_BV + 256].rearrange("p (h d) -> p h d", h=4)),
                           reads=tmk(tb, ["bv"]), writes=["vst%d_%db" % (sgb, i)])
                S.pool(lambda e: e.tensor_copy(out=v_stage[sgb][:, i, :, 64:65], in_=vcol_ap.unsqueeze(1).broadcast_to([128, 12, 1])),
                       reads=["valid", "cst"], writes=["vst%d_%dc" % (sgb, i)])

            def kside2(i, sgb, par, ikr=0):
                kk = "kbf%d" % par
                kb = cnt["tp"] % 2
                cnt["tp"] += 1
                for j in range(6):
                    S.pe(lambda e, j=j: e.transpose(tp_ps[kb][:, j, :], kbf[par][:, j, :], ident[:]),
                         reads=[kk + "a", kk + "b", "ident"], writes=["tp%d" % kb])
                S.pe(lambda e: e.transpose(tp_ps[kb][:, 6, :], ik4b[ikr][:], ident[:]),
                     reads=["ik4b%d" % ikr, "ident"], writes=["tp%d" % kb])
                S.act(lambda e: e.activation(out=kT_stage[sgb][:, :, i * 128:(i + 1) * 128], in_=tp_ps[kb][:, 0:7, :],
                                             func=AF.Copy), reads=["tp%d" % kb], writes=["kTst%d_%d" % (sgb, i)])

            def run_pipeline(tiles, preps):
                n = len(tiles)
                for idx in range(n + 2):
                    hook = tiles[idx - 1]["B1a"] if 0 <= idx - 1 < n else None
                    if idx < n:
                        tl = tiles[idx]
                        tl["A"](hook)
                        for pb_ in tl["prep_after"]:
                            preps[pb_]()
                    elif hook is not None:
                        hook()
                    if 0 <= idx - 1 < n:
                        tiles[idx - 1]["B1b"]()
                    if 0 <= idx - 2 < n:
                        tiles[idx - 2]["B2"]()

            npg = 2 * nslot
            gbs = {}
            tiles, preps = [], []
            for pg in range(npg):
                preps.append(lambda pg=pg: gbs.__setitem__(pg, proj_group(xT_all[:, pg * 512:(pg + 1) * 512], 4)))
            tcount = [0]
            for pg in range(npg):
                sgb = cnt["sg"] % 2
                cnt["sg"] += 1
                for i in range(4):
                    par = tcount[0] % 2
                    tcount[0] += 1
                    stt = {}

                    ikr = (tcount[0] - 1) % 4

                    def A(hook, pg=pg, i=i, stt=stt, sgb=sgb, ikr=ikr):
                        direct = {
                            "av": (v_stage[sgb][:, i, 0:8, 0:64], lambda pp: pp[:, 0:512].rearrange("p (h d) -> p h d", h=8),
                                   ["vst%d_%da" % (sgb, i)]),
                            "bv": (v_stage[sgb][:, i, 8:12, 0:64], lambda pp: pp[:, 0:256].rearrange("p (h d) -> p h d", h=4),
                                   ["vst%d_%db" % (sgb, i)]),
                            "ik": (ik4b[ikr][:].rearrange("p (a b) -> p a b", a=4),
                                   lambda pp: pp[:, 0:32].unsqueeze(1).broadcast_to([128, 4, 32]), ["ik4b%d" % ikr]),
                        }
                        stt["tb"] = proj_tile(gbs[pg], i, K_GROUPS, hook, direct)

                    def B1a(stt=stt, par=par):
                        norm_kk_a(stt["tb"], par)

                    def B1b(pg=pg, i=i, stt=stt, sgb=sgb, par=par, ikr=ikr):
                        norm_kk_b(stt["tb"], par)
                        kside1(stt["tb"], i, sgb, valid[:, pg * 4 + i:pg * 4 + i + 1], par, direct=True, ikr=ikr)

                    def B2(pg=pg, i=i, sgb=sgb, par=par, ikr=ikr):
                        kside2(i, sgb, par, ikr)
                        if i == 3:
                            S.dma(kT_s[:, :, pg * 512:(pg + 1) * 512], kT_stage[sgb][:], reads=kst_keys(sgb, 4),
                                  writes=["kT_s_p%d" % pg], semkey=("st", "kTst%d" % sgb))
                            S.dma(v_s[pg * 512:(pg + 1) * 512, :].rearrange("(i p) f -> p i f", p=128),
                                  v_stage[sgb][:].rearrange("p i h d -> p i (h d)"), reads=vst_keys(sgb, 4),
                                  writes=["v_s_p%d" % pg], semkey=("st", "vst%d" % sgb))

                    pa = [pg + 1] if (i == 1 and pg + 1 < npg) else []
                    tiles.append({"A": A, "B1a": B1a, "B1b": B1b, "B2": B2, "prep_after": pa})
            preps[0]()
            run_pipeline(tiles, preps)

            job_groups = []
            pj_ = list(range(nslot))
            for j0 in range(0, len(pj_), 4):
                job_groups.append(pj_[j0:j0 + 4])
            if nsamp:
                job_groups.append([NSLOT + b for b in range(nsamp)])
            gbs2 = {}
            tiles, preps = [], []
            for gi, jobs in enumerate(job_groups):
                preps.append(lambda gi=gi, jobs=jobs: gbs2.__setitem__(
                    gi, proj_group(xqT[:, jobs[0] * 128:(jobs[0] + len(jobs)) * 128], len(jobs))))
            for gi, jobs in enumerate(job_groups):
                for i, job in enumerate(jobs):
                    par = tcount[0] % 2
                    tcount[0] += 1
                    stt = {}
                    sgb = None
                    if job >= NSLOT:
                        sgb = cnt["sg"] % 2
                        cnt["sg"] += 1

                    def A(hook, gi=gi, i=i, stt=stt):
                        stt["tb"] = proj_tile(gbs2[gi], i, FULL_GROUPS, hook)

                    def B1a(stt=stt, par=par):
                        norm_kk_a(stt["tb"], par)

                    ikr = (tcount[0] - 1) % 4

                    def B1b(job=job, stt=stt, par=par, sgb=sgb, ikr=ikr):
                        tb = stt["tb"]
                        tmb = tm[tb]
                        norm_kk_b(tb, par)
                        if sgb is not None:
                            kside1(tb, 0, sgb, cst[:, 512:513], par, direct=False, ikr=ikr)
                        r0 = job * 128
                        stk = ("st", "tm%d" % tb)
                        S.dma(ak_o[r0:r0 + 128, :], tmb[:, C_AK:C_AK + 512], reads=tmk(tb, ["ak"]), semkey=stk)
                        S.dma(av_o[r0:r0 + 128, :], tmb[:, C_AV:C_AV + 512], reads=tmk(tb, ["av"]), semkey=stk)
                        S.dma(ik_o[r0:r0 + 128, :], tmb[:, C_IK:C_IK + 32], reads=tmk(tb, ["ik"]), semkey=stk)
                        S.dma(bk_o[r0:r0 + 128, :], tmb[:, C_BK:C_BK + 256], reads=tmk(tb, ["bk"]), semkey=stk)
                        S.dma(bv_o[r0:r0 + 128, :], tmb[:, C_BV:C_BV + 256], reads=tmk(tb, ["bv"]), semkey=stk)
                        norm_inplace(tb, "aq", C_AQ, 8, 64, G_AQ)
                        norm_inplace(tb, "bq", C_BQ, 8, 32, G_BQ)
                        norm_inplace(tb, "cq", C_CQ, 4, 64, G_CQ)
                        qb_ = qbf2[par]
                        qn = "qbf%d_" % par
                        S.dve(lambda e: e.tensor_copy(out=qb_[:, 0:4, :], in_=tmb[:, C_AQ:C_AQ + 512].rearrange("p (a b) -> p a b", a=4)),
                              reads=tmk(tb, ["aq"]), writes=[qn + "a"])
                        S.pool(lambda e: e.tensor_copy(out=qb_[:, 4:6, :], in_=tmb[:, C_CQ:C_CQ + 256].rearrange("p (a b) -> p a b", a=2)),
                               reads=tmk(tb, ["cq"]), writes=[qn + "c"])
                        S.pool(lambda e: e.tensor_copy(out=qb_[:, 6:8, :], in_=tmb[:, C_BQ:C_BQ + 256].rearrange("p (a b) -> p a b", a=2)),
                               reads=tmk(tb, ["bq"]), writes=[qn + "b"])
                        S.pool(lambda e: e.tensor_copy(out=qb_[:, 8:10, :], in_=tmb[:, C_IQ:C_IQ + 256].rearrange("p (a b) -> p a b", a=2)),
                               reads=tmk(tb, ["iq"]), writes=[qn + "i"])
                        S.act(lambda e: e.activation(out=tokb[:, 0:512], in_=tmb[:, C_AG:C_AG + 512], func=AF.Silu),
                              reads=tmk(tb, ["ag"]), writes=["tokb_a"])
                        S.act(lambda e: e.activation(out=tokb[:, 512:768], in_=tmb[:, C_BG:C_BG + 256], func=AF.Silu),
                              reads=tmk(tb, ["bg"]), writes=["tokb_b"])
                        S.act(lambda e: e.activation(out=tokb[:, 768:1024], in_=tmb[:, C_CG:C_CG + 256], func=AF.Silu),
                              reads=tmk(tb, ["cg"]), writes=["tokb_c"])
                        S.act(lambda e: e.activation(out=tokb[:, 1024:1032], in_=tmb[:, C_IW:C_IW + 8], func=AF.Copy),
                              reads=tmk(tb, ["iw"]), writes=["tokb_w"])
                        S.dma(tok_s[job], tokb[:], reads=["tokb_a", "tokb_b", "tokb_c", "tokb_w"], writes=["tok_s%d" % job],
                              semkey=("st", "tokb"))

                    def B2(job=job, par=par, sgb=sgb, ikr=ikr):
                        if sgb is not None:
                            kside2(0, sgb, par, ikr)
                            b = job - NSLOT
                            base = P_ROWS + b * S_ROWS + S_ROWS - 64
                            S.dma(kT_s[:, :, base:base + 64], kT_stage[sgb][:, :, 64:128], reads=["kTst%d_0" % sgb, "kT_s_r%d_2" % b],
                                  writes=["kT_s_n%d" % b], semkey=("st", "kTst%d" % sgb))
                            S.dma(v_s[base:base + 64, :], v_stage[sgb][64:128, 0, :, :].rearrange("p h d -> p (h d)"),
                                  reads=vst_keys(sgb, 1) + ["v_s_r%d_2" % b], writes=["v_s_n%d" % b], semkey=("st", "vst%d" % sgb))
                        qb_ = qbf2[par]
                        qn = "qbf%d_" % par
                        kb = cnt["tp"] % 2
                        kb2 = (cnt["tp"] + 1) % 2
                        cnt["tp"] += 2
                        for jx in range(6):
                            S.pe(lambda e, jx=jx: e.transpose(tp_ps[kb][:, jx, :], qb_[:, jx, :], ident[:]),
                                 reads=[qn + "a", qn + "c", "ident"], writes=["tp%d" % kb])
                        for jx in range(4):
                            S.pe(lambda e, jx=jx: e.transpose(tp_ps[kb2][:, jx, :], qb_[:, 6 + jx, :], ident[:]),
                                 reads=[qn + "b", qn + "i", "ident"], writes=["tp%d" % kb2])
                        S.dve(lambda e: e.tensor_copy(out=qpad[0:64, 0:8:2, :], in_=tp_ps[kb][0:64, 0:4, :]),
                              reads=["tp%d" % kb], writes=["qpad"])
                        S.dve(lambda e: e.tensor_copy(out=qpad[64:128, 1:8:2, :], in_=tp_ps[kb][64:128, 0:4, :]),
                              reads=["tp%d" % kb], writes=["qpad"])
                        S.dve(lambda e: e.tensor_copy(out=qpad[0:64, 24:28:2, :], in_=tp_ps[kb][0:64, 4:6, :]),
                              reads=["tp%d" % kb], writes=["qpad"])
                        S.dve(lambda e: e.tensor_copy(out=qpad[64:128, 25:28:2, :], in_=tp_ps[kb][64:128, 4:6, :]),
                              reads=["tp%d" % kb], writes=["qpad"])
                        for m in range(4):
                            S.act(lambda e, m=m: e.activation(out=qpad[32 * m:32 * m + 32, 8 + m:16:4, :],
                                                              in_=tp_ps[kb2][32 * m:32 * m + 32, 0:2, :], func=AF.Copy),
                                  reads=["tp%d" % kb2], writes=["qpad"])
                            S.act(lambda e, m=m: e.activation(out=qpad[32 * m:32 * m + 32, 16 + m:24:4, :],
                                                              in_=tp_ps[kb2][32 * m:32 * m + 32, 2:4, :], func=AF.Copy),
                                  reads=["tp%d" % kb2], writes=["qpad"])
                        S.dma(q_s[job], qpad[:], reads=["qpad"], writes=["q_s%d" % job], semkey=("st", "qpad"))

                    pa = [gi + 1] if (i == min(1, len(jobs) - 1) and gi + 1 < len(job_groups)) else []
                    tiles.append({"A": A, "B1a": B1a, "B1b": B1b, "B2": B2, "prep_after": pa})
            preps[0]()
            run_pipeline(tiles, preps)

        if do_attn:
            S.barrier()
            _phase2(nc, S, sb, ps, dict(nslot=nslot, nsamp=nsamp, niter=niter, kT_s=kT_s, v_s=v_s, mk_s=mk_s, mv_s=mv_s,
                                        q_s=q_s, tok_s=tok_s, ident=ident, gains=gains, cst=cst, xq=xq, y_o=y_o,
                                        fmask_d=fmask_d, lamv_d=lamv_d, tabp_d=tabp_d, oneh_d=oneh_d, tvec=tvec,
                                        w_out=w_out, dbg_o=dbg_o, wout_s=wout_s))
        S.emit()
    return nc


def _phase2(nc, S, sb, ps, L):
    nslot, nsamp, niter = L["nslot"], L["nsamp"], L["niter"]
    kT_s, v_s, mk_s, mv_s, q_s, tok_s = L["kT_s"], L["v_s"], L["mk_s"], L["mv_s"], L["q_s"], L["tok_s"]
    ident, gains, cst, xq, y_o, tvec = L["ident"], L["gains"], L["cst"], L["xq"], L["y_o"], L["tvec"]
    wout_s = L["wout_s"]
    NMAX = 128 * 128
    CH = 2048
    NCHMAX = NMAX // CH
    U8 = mybir.dt.uint8
    with contextlib.ExitStack() as st2:
        Jm = sb("Jm", [128, 128], BF16, st2)
        dmask_bf = sb("dmask_bf", [128, 128], BF16, st2)
        zer_bf = sb("zer_bf", [128, 512], BF16, st2)
        fmask = sb("fmask", [128, 2 * 896], BF16, st2)
        lam = sb("lam", [128, 4], F32, st2)
        biasR = sb("biasR", [128, 2, 12, 2, 128], BF16, st2)
        wh = sb("wh", [128, 8, 512], BF16, st2)
        qpi = [sb("qpi%d" % i, [128, 8, 128], BF16, st2) for i in range(2)]
        qpa = sb("qpa", [128, 20, 128], BF16, st2)
        iw = [sb("iw%d" % i, [128, 24], F32, st2) for i in range(2)]
        gate = sb("gate", [128, 1024], F32, st2)
        score = sb("score", [128, NMAX], F32, st2)
        negm = sb("negm", [128, NMAX], BF16, st2)
        junk = [sb("junk%d" % i, [128, CH], U8, st2) for i in range(2)]
        ikg = [sb("ikg%d" % i, [128, 512], BF16, st2) for i in range(3)]
        rlb = [sb("rlb%d" % i, [128, 512], BF16, st2) for i in range(3)]
        Dg = [sb("Dg%d" % i, [128, 8, 128], BF16, st2) for i in range(2)]
        pw = sb("pw", [128, niter], F32, st2)
        wk = [sb("wk%d" % i, [128, niter], F32, st2) for i in range(2)]
        kg = [sb("kg%d" % i, [128, 6, 512], BF16, st2) for i in range(2)]
        vg = [sb("vg%d" % i, [128, 4, 780], BF16, st2) for i in range(2)]
        mkg = sb("mkg", [128, 2, 256], BF16, st2)
        mvg = sb("mvg", [128, 2, 260], BF16, st2)
        PT = [sb("PT%d" % i, [128, 4, 128], BF16, st2) for i in range(4)]
        bs = [sb("bs%d" % i, [128, 8], F32, st2) for i in range(2)]
        cntc = [sb("cntc%d" % i, [128, niter * NCHMAX + 8], F32, st2) for i in range(2)]
        fin = sb("fin", [128, 64], F32, st2)
        ocat = sb("ocat", [128, 1024], F32, st2)
        oT = sb("oT", [128, 8, 128], BF16, st2)
        xres = sb("xres", [128, 1024], F32, st2)
        sqd = sb("sqd", [128, 256], F32, st2)
        stp = [ps("stp%d" % i, [128, 4, 128], F32, st2) for i in range(2)]
        acc = [ps("acc%d" % i, [128, 512], F32, st2) for i in range(3)]
        p1ps = [ps("p1ps%d" % i, [128, 512], F32, st2) for i in range(2)]
        p1acc = ps("p1acc", [128, 512], F32, st2)

        S.dma(score[:, 0:1792], L["fmask_d"], writes=["wst0"])
        S.dve(lambda e: e.tensor_copy(out=fmask[:], in_=score[:, 0:1792]), reads=["wst0"], writes=["fmask"])
        S.dve(lambda e: e.tensor_copy(out=Jm[:], in_=cst[:, 128:256]), reads=["cst"], writes=["Jm"])
        S.dve(lambda e: e.tensor_copy(out=dmask_bf[:], in_=cst[:, 384:512]), reads=["cst"], writes=["dmask_bf"])
        S.dve(lambda e: e.memset(zer_bf[:], 0.0), writes=["zer_bf"])
        for k in range(niter):
            S.dve(lambda e, k=k: e.memset(pw[:, k:k + 1], 2.0 ** -(k + 1)), writes=["pw"])
        wst = negm[:, 0:8192].rearrange("p (c n) -> p c n", c=8)
        for c in range(8):
            cc = 1 + c % 3
            S.dma(score[:, cc * 2048:cc * 2048 + 1024], L["w_out"][c * 128:(c + 1) * 128, :], writes=["wst%d" % cc])
            S.pool(lambda e, c=c, cc=cc: e.tensor_copy(out=wst[:, c, :], in_=score[:, cc * 2048:cc * 2048 + 1024]),
                   reads=["wst%d" % cc], writes=["negm"])
        S.dma(wout_s, wst, reads=["negm"], writes=["wout_s"], semkey=("st", "negm"))
        lamv = score[:, 8192:8320]
        lamt = score[:, 8320:8384]
        S.dma(lamv, L["lamv_d"], writes=["lamv"])
        S.dve(lambda e: e.tensor_tensor(out=lamt[:, 0:32], in0=lamv[:, 0:32], in1=lamv[:, 32:64], op=ALU.mult),
              reads=["lamv"], writes=["lamt0"])
        S.dve(lambda e: e.tensor_tensor(out=lamt[:, 32:64], in0=lamv[:, 64:96], in1=lamv[:, 96:128], op=ALU.mult),
              reads=["lamv"], writes=["lamt1"])
        S.dve(lambda e: e.tensor_reduce(out=lam[:, 0:2], in_=lamt.rearrange("p (a b) -> p a b", a=2),
                                        axis=AX.X, op=ALU.add), reads=["lamt0", "lamt1"], writes=["lam01"])
        S.act(lambda e: e.activation(out=lam[:, 2:4], in_=lam[:, 0:2], func=AF.Exp), reads=["lam01"], writes=["lam23"])
        S.dve(lambda e: e.tensor_tensor(out=lam[:, 0:1], in0=lam[:, 3:4], in1=lam[:, 2:3], op=ALU.subtract),
              reads=["lam23", "lam01"], writes=["lam0"])
        S.dve(lambda e: e.tensor_scalar(out=lam[:, 1:2], in0=lam[:, 0:1], scalar1=-0.2, scalar2=None, op0=ALU.add),
              reads=["lam0"], writes=["neglam"])
        tabp = score[:, 8400:8528]
        oneh = score[:, 8600:8984]
        tv_sb = score[:, 9000:9384]
        bR = score[:, 9400:12472].rearrange("p (d h j) -> p d h j", d=2, h=12)
        bHf = score[:, 12500:15572].rearrange("p (d h j) -> p d h j", d=2, h=12)
        bHi = kg[0][:].rearrange("p a b -> p (a b)").rearrange("p (d h j) -> p d h j", d=2, h=12)
        tv_ps = acc[0][:, 0:384]
        S.dma(tabp, L["tabp_d"], writes=["tabp"])
        S.dma(oneh, L["oneh_d"], writes=["oneh"])
        S.pe(lambda e: e.matmul(tv_ps, lhsT=tabp, rhs=oneh, start=True, stop=True), reads=["tabp", "oneh"], writes=["acc0"])
        S.dve(lambda e: e.tensor_copy(out=tv_sb, in_=tv_ps), reads=["acc0"], writes=["tv_sb"])
        S.dma(tvec, tv_sb[0:12, :], reads=["tv_sb"], writes=["tvec"])
        for d in range(2):
            src = bass.AP(tvec.tensor, 128 - 128 * d, [[1, 128], [384, 12], [1, 128]])
            S.dma(bR[:, d, :, :], src, reads=["tvec"], writes=["bR%d" % d])
        S.dve(lambda e: e.tensor_scalar(out=bR[:, :, 0:8, :], in0=bR[:, :, 0:8, :], scalar1=8.0, scalar2=None, op0=ALU.mult),
              reads=["bR0", "bR1"], writes=["bR0", "bR1"])
        S.dve(lambda e: e.tensor_scalar(out=bR[:, :, 8:12, :], in0=bR[:, :, 8:12, :], scalar1=32.0 ** 0.5, scalar2=None, op0=ALU.mult),
              reads=["bR0", "bR1"], writes=["bR0", "bR1"])
        S.dve(lambda e: e.tensor_copy(out=bHi, in_=bR), reads=["bR0", "bR1"], writes=["kg0"])
        S.dve(lambda e: e.tensor_copy(out=biasR[:, :, :, 0, :], in_=bHi), reads=["kg0"], writes=["biasR0"])
        S.dve(lambda e: e.tensor_copy(out=bHf, in_=bHi), reads=["kg0"], writes=["bHf"])
        S.dve(lambda e: e.tensor_tensor(out=bHf, in0=bR, in1=bHf, op=ALU.subtract), reads=["bHf", "bR0", "bR1"], writes=["bHf"])
        S.dve(lambda e: e.tensor_copy(out=biasR[:, :, :, 1, :], in_=bHf), reads=["bHf"], writes=["biasR1"])
        SETUP_KEYS = ["wst%d" % c for c in range(4)] + ["lamv", "lamt0", "lamt1", "tabp", "oneh", "tv_sb", "bR0", "bR1", "bHf"]

        def acc_slot(kind, idx):
            if kind == "a":
                return (0, idx * 65) if idx < 7 else (1, 0)
            if kind == "d":
                h, c = idx // 2, idx % 2
                return (1, (1 + h) * 65) if c == 0 else (2, h * 65)
            return (1, (5 + idx) * 65) if idx < 2 else (2, (2 + idx) * 65)

        jobs = [(s, 0, 8 * s + 8, 0, 0) for s in reversed(range(nslot))]
        jobs += [(NSLOT + b, P_ROWS + b * S_ROWS, 12, 1, 1 + b) for b in range(nsamp)]
        state = {"p1": 0, "u": 0, "first": True}

        def P1(jn):
            job, base, T, fm, memr = jobs[jn]
            jb = jn % 2
            N, G = T * 128, T // 4
            bsb, cc_ = bs[jb], cntc[jb]
            B = "bs%d_" % jb
            S.dma(qpi[jb][:], q_s[job, :, 16:24, :], reads=["q_s%d" % job], writes=["qpi%d" % jb])
            S.dma(iw[jb][:, 0:8], tok_s[job, :, 1024:1032], reads=["tok_s%d" % job], writes=["iw%d" % jb])
            for h in range(8):
                S.dve(lambda e, h=h: e.tensor_scalar(out=Dg[jb][:, h, :], in0=ident[:], scalar1=iw[jb][:, h:h + 1], scalar2=None,
                                                     op0=ALU.mult), reads=["ident", "iw%d" % jb], writes=["Dg%d" % jb])
            yield
            pend = None
            for g in range(G):
                ib = state["p1"] % 3
                S.dma(ikg[ib][:], kT_s[:, 6, base + g * 512: base + (g + 1) * 512],
                      reads=_scr_keys("kT", base, g), writes=["ikg%d" % ib])
                for h in range(8):
                    rb = (state["p1"] * 8 + h) % 3
                    pq = (state["p1"] * 8 + h) % 2
                    S.pe(lambda e, h=h, ib=ib, pq=pq: e.matmul(p1ps[pq][:], lhsT=qpi[jb][:, h, :], rhs=ikg[ib][:], start=True, stop=True),
                         reads=["qpi%d" % jb, "ikg%d" % ib], writes=["p1ps%d" % pq])
                    if h % 2 == 0:
                        S.act(lambda e, rb=rb, pq=pq: e.activation(out=rlb[rb][:], in_=p1ps[pq][:], func=AF.Relu),
                              reads=["p1ps%d" % pq], writes=["rlb%d" % rb])
                    else:
                        S.dve(lambda e, rb=rb, pq=pq: e.tensor_scalar(out=rlb[rb][:], in0=p1ps[pq][:], scalar1=0.0, scalar2=None,
                                                                      op0=ALU.max), reads=["p1ps%d" % pq], writes=["rlb%d" % rb])
                    if pend is not None:
                        pend()
                    extra = SETUP_KEYS if state["first"] else []

                    def accmm(h=h, rb=rb, g=g, extra=extra):
                        S.pe(lambda e: e.matmul(p1acc[:], lhsT=Dg[jb][:, h, :], rhs=rlb[rb][:], start=(h == 0), stop=(h == 7)),
                             reads=["Dg%d" % jb, "rlb%d" % rb], writes=["p1acc"])
                        if h == 7:
                            S.dve(lambda e: e.tensor_copy(out=score[:, g * 512:(g + 1) * 512], in_=p1acc[:]),
                                  reads=["p1acc"], writes=["score%d" % g] + extra)
                    pend = accmm
                    yield "hs"
                state["p1"] += 1
            if pend is not None:
                pend()
            state["first"] = False
            skeys = ["score%d" % g for g in range(G)]
            S.dve(lambda e: e.tensor_reduce(out=bsb[:, 1:2], in_=score[:, 0:N], axis=AX.X, op=ALU.max), reads=skeys, writes=[B + "hi"])
            S.dve(lambda e: e.tensor_reduce(out=bsb[:, 0:1], in_=score[:, 0:N], axis=AX.X, op=ALU.min), reads=skeys, writes=[B + "lo"])
            S.dve(lambda e: e.tensor_scalar(out=bsb[:, 0:1], in0=bsb[:, 0:1], scalar1=-1.0, scalar2=None, op0=ALU.add),
                  reads=[B + "lo"], writes=[B + "lo"])
            S.dve(lambda e: e.tensor_tensor(out=bsb[:, 3:4], in0=bsb[:, 1:2], in1=bsb[:, 0:1], op=ALU.subtract),
                  reads=[B + "lo", B + "hi"], writes=[B + "w0"])
            S.dve(lambda e: e.tensor_scalar(out=wk[jb][:], in0=pw[:], scalar1=bsb[:, 3:4], scalar2=None, op0=ALU.mult),
                  reads=[B + "w0", "pw"], writes=[B + "wk"])
            S.dve(lambda e: e.tensor_tensor(out=bsb[:, 2:3], in0=bsb[:, 0:1], in1=wk[jb][:, 0:1], op=ALU.add),
                  reads=[B + "lo", B + "wk"], writes=[B + "mid"])
            S.dve(lambda e: e.tensor_tensor(out=score[:, 0:896], in0=score[:, 0:896], in1=fmask[:, fm * 896:(fm + 1) * 896],
                                            op=ALU.add), reads=skeys[0:2] + ["fmask"], writes=skeys[0:2])
            S.dve(lambda e: e.tensor_tensor(out=score[:, N - 128:N], in0=score[:, N - 128:N], in1=cst[:, 256:384], op=ALU.add),
                  reads=[skeys[-1], "cst"], writes=[skeys[-1]])
            S.dve(lambda e: e.memset(cc_[:], 0.0), writes=[B + "cnt"] + [B + "cnt%d" % c for c in range(NCHMAX)])
            yield
            nch = (N + CH - 1) // CH
            for it in range(niter):
                for c in range(nch):
                    w = min(CH, N - c * CH)
                    col = it * NCHMAX + c
                    jk = c % 2
                    S.dve(lambda e, c=c, w=w, col=col, jk=jk: e.tensor_scalar(
                        out=junk[jk][:, 0:w], in0=score[:, c * CH:c * CH + w], scalar1=bsb[:, 2:3], scalar2=0.0,
                        op0=ALU.is_gt, op1=ALU.add, accum_out=cc_[:, col:col + 1]),
                        reads=skeys[c * 4:c * 4 + 4] + [B + "mid", B + "cnt"], writes=["junk%d" % jk, B + "cnt%d" % c])
                    if c % 2 == 1:
                        yield
                ckeys = [B + "cnt%d" % c for c in range(nch)]
                if nch > 1:
                    S.dve(lambda e, it=it: e.tensor_reduce(out=bsb[:, 6:7], in_=cc_[:, it * NCHMAX:it * NCHMAX + nch], axis=AX.X, op=ALU.add),
                          reads=ckeys, writes=[B + "tot"])
                    tot, totk = bsb[:, 6:7], [B + "tot"]
                else:
                    tot, totk = cc_[:, it * NCHMAX:it * NCHMAX + 1], ckeys
                S.dve(lambda e, it=it, tot=tot: e.scalar_tensor_tensor(out=bsb[:, 4:5], in0=tot, scalar=255.5, in1=wk[jb][:, it:it + 1],
                                                                       op0=ALU.is_gt, op1=ALU.mult),
                      reads=totk + [B + "wk"], writes=[B + "tmp"])
                if it + 1 < niter:
                    S.dve(lambda e, it=it: e.scalar_tensor_tensor(out=bsb[:, 2:3], in0=bsb[:, 4:5], scalar=wk[jb][:, it + 1:it + 2],
                                                                  in1=bsb[:, 0:1], op0=ALU.add, op1=ALU.add),
                          reads=[B + "tmp", B + "wk", B + "lo"], writes=[B + "mid"])
                S.dve(lambda e: e.tensor_tensor(out=bsb[:, 0:1], in0=bsb[:, 0:1], in1=bsb[:, 4:5], op=ALU.add),
                      reads=[B + "tmp", B + "lo"], writes=[B + "lo"])
                yield

        def P2(jn):
            job, base, T, fm, memr = jobs[jn]
            jb = jn % 2
            N, G = T * 128, T // 4
            B = "bs%d_" % jb
            skeys = ["score%d" % g for g in range(G)]
            S.dve(lambda e: e.tensor_scalar(out=negm[:, 0:N], in0=score[:, 0:N], scalar1=bs[jb][:, 0:1], scalar2=NEGM,
                                            op0=ALU.is_le, op1=ALU.mult), reads=skeys + [B + "lo"], writes=["negm"])
            S.dma(qpa[:, 0:16, :], q_s[job, :, 0:16, :], reads=["q_s%d" % job], writes=["qpa"])
            S.dma(qpa[:, 16:20, :], q_s[job, :, 24:28, :], reads=["q_s%d" % job], writes=["qpa2"])
            S.dma(mkg[:], mk_s[memr], reads=["mk_s%d" % memr], writes=["mkg"])
            S.dma(mvg[:], mv_s[memr].rearrange("(i p) f -> p i f", p=128), reads=["mv_s%d" % memr], writes=["mvg"])
            S.dma(gate[:], tok_s[job, :, 0:1024], reads=["tok_s%d" % job], writes=["gate"])
            S.dma(xres[:], xq[job * 128:(job + 1) * 128, :], writes=["xres"])
            S.dma(wh[:], wout_s[:, :, 0:512], reads=["wout_s"], writes=["wh"])
            for a in range(3):
                S.pe(lambda e, a=a: e.matmul(acc[a][:], lhsT=zer_bf[:, 0:128], rhs=zer_bf[:], start=True, stop=False),
                     reads=["zer_bf"], writes=["acc%d" % a])
            yield
            QK = ["qpa", "qpa2"]
            units = []
            for t in range(T):
                for hf in range(2):
                    units.append(("a", t, hf))
                for hf in range(2):
                    units.append(("d", t, hf))
            units.append(("m", 0, 0))
            units.append(("m", 1, 0))
            pending = []

            def emit_qk(u, kind, t, hf):
                sb_ = u % 2
                stt, stk = stp[sb_], "stp%d" % sb_
                g, i = t // 4, t % 4
                gbuf = g % 2
                if kind == "a" and i == 0 and hf == 0:
                    S.dma(kg[gbuf][:], kT_s[:, 0:6, base + g * 512: base + (g + 1) * 512],
                          reads=_scr_keys("kT", base, g), writes=["kg%d" % gbuf])
                    S.dma(vg[gbuf][:], v_s[base + g * 512: base + (g + 1) * 512, :].rearrange("(i p) f -> p i f", p=128),
                          reads=_scr_keys("v", base, g), writes=["vg%d" % gbuf])
                near = None
                if kind in ("a", "d"):
                    if t == T - 1:
                        near = 0
                    elif t == T - 2:
                        near = 1
                if kind == "a":
                    for hh in range(4):
                        h = 4 * hf + hh
                        S.pe(lambda e, hh=hh, t=t: e.matmul(stt[:, hh, :], lhsT=negm[:, t * 128:(t + 1) * 128], rhs=ident[:],
                                                            start=True, stop=False),
                             reads=["negm", "ident"], writes=[stk])
                        if near is not None:
                            for hl in range(2):
                                S.pe(lambda e, hh=hh, h=h, hl=hl: e.matmul(stt[:, hh, :], lhsT=biasR[:, near, h, hl, :], rhs=Jm[:],
                                                                           start=False, stop=False),
                                     reads=["biasR0", "biasR1", "Jm"], writes=[stk])
                        S.pe(lambda e, hh=hh, h=h, i=i: e.matmul(stt[:, hh, :], lhsT=kg[gbuf][:, h // 2, i * 128:(i + 1) * 128],
                                                                 rhs=qpa[:, h, :], start=False, stop=True),
                             reads=["kg%d" % gbuf] + QK, writes=[stk])
                    scale = 0.125
                elif kind == "d":
                    for hh in range(4):
                        p = 4 * hf + hh
                        first = True
                        if near == 0:
                            S.pe(lambda e, hh=hh: e.matmul(stt[:, hh, :], lhsT=dmask_bf[:], rhs=ident[:], start=True, stop=False),
                                 reads=["dmask_bf", "ident"], writes=[stk])
                            first = False
                        if near is not None:
                            for hl in range(2):
                                S.pe(lambda e, hh=hh, p=p, hl=hl, first=first: e.matmul(
                                    stt[:, hh, :], lhsT=biasR[:, near, 8 + p // 2, hl, :], rhs=Jm[:],
                                    start=(first and hl == 0), stop=False),
                                    reads=["biasR0", "biasR1", "Jm"], writes=[stk])
                            first = False
                        S.pe(lambda e, hh=hh, p=p, i=i, first=first: e.matmul(
                            stt[:, hh, :], lhsT=kg[gbuf][:, 4 + p // 4, i * 128:(i + 1) * 128],
                            rhs=qpa[:, 8 + p, :], start=first, stop=True),
                            reads=["kg%d" % gbuf] + QK, writes=[stk])
                    scale = 32.0 ** -0.5
                else:
                    for h in range(4):
                        S.pe(lambda e, h=h, t=t: e.matmul(stt[:, h, :], lhsT=mkg[:, h // 2, t * 128:(t + 1) * 128],
                                                          rhs=qpa[:, 16 + h, :], start=True, stop=True),
                             reads=["mkg"] + QK, writes=[stk])
                    scale = 0.125
                pb = u % 4
                S.act(lambda e, scale=scale, pb=pb: e.activation(out=PT[pb][:], in_=stt[:], func=AF.Exp, scale=scale),
                      reads=[stk], writes=["PT%d" % pb])

            def emit_pv(u, kind, t, hf):
                pb = u % 4
                g, i = t // 4, t % 4
                gbuf = g % 2
                for hh in range(4):
                    if kind == "a":
                        h = 4 * hf + hh
                        a, col = acc_slot("a", h)
                        rhs = vg[gbuf][:, i, h * 65:(h + 1) * 65]
                        rk = "vg%d" % gbuf
                    elif kind == "d":
                        p = 4 * hf + hh
                        a, col = acc_slot("d", p)
                        vh = 8 + p // 2
                        rhs = vg[gbuf][:, i, vh * 65:(vh + 1) * 65]
                        rk = "vg%d" % gbuf
                    else:
                        a, col = acc_slot("m", hh)
                        rhs = mvg[:, t, hh * 65:(hh + 1) * 65]
                        rk = "mvg"
                    last = ((kind == "a" and t == T - 1 and hf == 1 and hh == 2) or
                            (kind == "m" and t == 1 and hh in (1, 3)))
                    S.pe(lambda e, hh=hh, a=a, col=col, rhs=rhs, last=last: e.matmul(
                        acc[a][:, col:col + 65], lhsT=PT[pb][:, hh, :], rhs=rhs, start=False, stop=last),
                        reads=["PT%d" % pb, rk], writes=["acc%d" % a])

            for (kind, t, hf) in units:
                u = state["u"]
                state["u"] += 1
                emit_qk(u, kind, t, hf)
                if len(pending) >= 2:
                    emit_pv(*pending.pop(0))
                pending.append((u, kind, t, hf))
                yield
            while pending:
                emit_pv(*pending.pop(0))
            A = ["acc0", "acc1", "acc2"]
            a3 = [acc[i][:, 0:455].rearrange("p (h d) -> p h d", d=65) for i in range(3)]
            S.dve(lambda e: e.tensor_copy(out=fin[:, 0:7].unsqueeze(2), in_=a3[0][:, 0:7, 64:65]), reads=[A[0]], writes=["fin_d"])
            S.dve(lambda e: e.tensor_copy(out=fin[:, 7:14].unsqueeze(2), in_=a3[1][:, 0:7, 64:65]), reads=[A[1]], writes=["fin_d"])
            S.dve(lambda e: e.tensor_copy(out=fin[:, 14:20].unsqueeze(2), in_=a3[2][:, 0:6, 64:65]), reads=[A[2]], writes=["fin_d"])
            S.dve(lambda e: e.reciprocal(out=fin[:, 20:40], in_=fin[:, 0:20]), reads=["fin_d"], writes=["fin_r"])

            def bc(c0, n):
                return fin[:, 20 + c0:20 + c0 + n].unsqueeze(2).broadcast_to([128, n, 64])

            def o3(c0, n):
                return ocat[:, c0:c0 + 64 * n].rearrange("p (h d) -> p h d", h=n)

            S.dve(lambda e: e.tensor_tensor(out=o3(0, 7), in0=a3[0][:, 0:7, 0:64], in1=bc(0, 7), op=ALU.mult),
                  reads=[A[0], "fin_r"], writes=["ocat_a0"])
            S.dve(lambda e: e.tensor_tensor(out=o3(448, 1), in0=a3[1][:, 0:1, 0:64], in1=bc(7, 1), op=ALU.mult),
                  reads=[A[1], "fin_r"], writes=["ocat_a1"])
            S.dve(lambda e: e.tensor_tensor(out=o3(512, 4), in0=a3[1][:, 1:5, 0:64], in1=bc(8, 4), op=ALU.mult),
                  reads=[A[1], "fin_r"], writes=["ocat_b"])
            S.dve(lambda e: e.tensor_tensor(out=sqd[:].rearrange("p (h d) -> p h d", h=4), in0=a3[2][:, 0:4, 0:64], in1=bc(14, 4),
                                            op=ALU.mult), reads=[A[2], "fin_r"], writes=["sqd"])
            S.dve(lambda e: e.tensor_tensor(out=o3(768, 2), in0=a3[1][:, 5:7, 0:64], in1=bc(12, 2), op=ALU.mult),
                  reads=[A[1], "fin_r"], writes=["ocat_c0"])
            S.dve(lambda e: e.tensor_tensor(out=o3(896, 2), in0=a3[2][:, 4:6, 0:64], in1=bc(18, 2), op=ALU.mult),
                  reads=[A[2], "fin_r"], writes=["ocat_c1"])
            yield
            S.dve(lambda e: e.scalar_tensor_tensor(out=ocat[:, 512:768], in0=sqd[:], scalar=lam[:, 1:2], in1=ocat[:, 512:768],
                                                   op0=ALU.mult, op1=ALU.add), reads=["sqd", "ocat_b", "neglam"], writes=["ocat_b"])
            S.dve(lambda e: e.tensor_tensor(out=sqd[:], in0=ocat[:, 512:768], in1=ocat[:, 512:768], op=ALU.mult),
                  reads=["ocat_b"], writes=["sqd"])
            S.dve(lambda e: e.tensor_reduce(out=fin[:, 40:44], in_=sqd[:].rearrange("p (h d) -> p h d", h=4), axis=AX.X, op=ALU.add),
                  reads=["sqd"], writes=["fin_s"])
            S.act(lambda e: e.activation(out=fin[:, 44:48], in_=fin[:, 40:44], func=AF.Sqrt, bias=EPS, scale=1.0 / 64),
                  reads=["fin_s"], writes=["fin_q"])
            S.dve(lambda e: e.reciprocal(out=fin[:, 48:52], in_=fin[:, 44:48]), reads=["fin_q"], writes=["fin_rs"])
            S.dve(lambda e: e.tensor_tensor(out=o3(512, 4), in0=o3(512, 4),
                                            in1=fin[:, 48:52].unsqueeze(2).broadcast_to([128, 4, 64]), op=ALU.mult),
                  reads=["fin_rs", "ocat_b"], writes=["ocat_b"])
            S.dve(lambda e: e.tensor_tensor(out=ocat[:, 512:768], in0=ocat[:, 512:768], in1=gains[:, G_SUB:G_SUB + 256], op=ALU.mult),
                  reads=["gains", "ocat_b"], writes=["ocat_b"])
            OK = ["ocat_a0", "ocat_a1", "ocat_b", "ocat_c0", "ocat_c1"]
            S.dve(lambda e: e.tensor_tensor(out=ocat[:], in0=ocat[:], in1=gate[:], op=ALU.mult), reads=OK + ["gate"], writes=OK)
            if L["dbg_o"] is not None:
                S.dma(L["dbg_o"][job * 128:(job + 1) * 128, :], ocat[:], reads=OK, semkey=("st", "ocat"))
            for hf in range(2):
                for c in range(4):
                    S.pe(lambda e, c=c, hf=hf: e.transpose(stp[hf][:, c, :], ocat[:, (4 * hf + c) * 128:(4 * hf + c + 1) * 128], cst[:, 0:128]),
                         reads=OK + ["cst"], writes=["stp%d" % hf])
                S.act(lambda e, hf=hf: e.activation(out=oT[:, 4 * hf:4 * hf + 4, :], in_=stp[hf][:], func=AF.Copy),
                      reads=["stp%d" % hf], writes=["oT%d" % hf])
            yield
            for half in range(2):
                yb = stp[half][:].rearrange("p h q -> p (h q)")
                ybk = "stp%d" % half
                if half == 1:
                    S.dma(wh[:], wout_s[:, :, 512:1024], reads=["wout_s"], writes=["wh"])
                for c in range(8):
                    S.pe(lambda e, c=c, yb=yb: e.matmul(yb, lhsT=oT[:, c, :], rhs=wh[:, c, :], start=(c == 0), stop=(c == 7)),
                         reads=["oT0", "oT1", "wh"], writes=[ybk])
                S.dve(lambda e, half=half, yb=yb: e.tensor_tensor(out=xres[:, half * 512:(half + 1) * 512], in0=yb,
                                                                  in1=xres[:, half * 512:(half + 1) * 512], op=ALU.add),
                      reads=[ybk, "xres"], writes=["xres"])
            S.dma(y_o[job * 128:(job + 1) * 128, :], xres[:], reads=["xres"], semkey=("st", "xres"))
            yield

        def steps_p1(jn):
            job, base, T, fm, memr = jobs[jn]
            nch = (T * 128 + CH - 1) // CH
            return 2 + 8 * (T // 4) + niter * (1 + nch // 2)

        def steps_p2(jn):
            return 1 + 2 * jobs[jn][2] + 2 + 3

        for _ in P1(0):
            pass
        for jn in range(len(jobs)):
            g2 = P2(jn)
            g1 = P1(jn + 1) if jn + 1 < len(jobs) else None
            alive1 = g1 is not None
            for _ in g2:
                if alive1:
                    try:
                        tag = next(g1)
                        if tag == "hs":
                            next(g1)
                    except StopIteration:
                        alive1 = False
            if alive1:
                for _ in g1:
                    pass


def _scr_keys(which, base, g):
    if base < P_ROWS:
        return ["%s_s_p%d" % (which, g)]
    b = (base - P_ROWS) // S_ROWS
    ks = ["%s_s_r%d_%d" % (which, b, g)]
    if g == 2:
        ks.append("%s_s_n%d" % (which, b))
    return ks


def _rel_bucket_np(rel):
    nb = 16
    ret = np.where(rel > 0, nb, 0)
    n = np.abs(rel)
    max_exact = 8
    nf = np.maximum(n, 1).astype(np.float32)
    large = max_exact + (np.log(nf / max_exact) / math.log(128 / max_exact) * (nb - max_exact)).astype(np.int32)
    large = np.minimum(large, nb - 1)
    return ret + np.where(n < max_exact, n, large)


def _host_consts():
    cstv = np.zeros((128, 640), np.float32)
    cstv[:, 0:128] = np.eye(128, dtype=np.float32)
    cstv[:, 128:256] = np.eye(128, dtype=np.float32)[::-1]
    q = np.arange(128)[:, None]
    k = np.arange(128)[None, :]
    adm = (k // 64) <= (q // 64)
    cstv[:, 256:384] = np.where(adm, 0.0, -1e30)
    cstv[:, 384:512] = np.where(adm, 0.0, NEGM)
    cstv[0:64, 512] = 0.0
    cstv[64:128, 512] = 1.0
    oneh = np.zeros((128, 384), np.float32)
    rel = np.arange(384) - 255
    bk = _rel_bucket_np(rel)
    oneh[bk[:383], np.arange(383)] = 1.0
    oneh[15, :383] -= 1.0
    return cstv, oneh


_NC_CACHE = {}


def _prep_inputs(x_prompt, x_sample, mem_prompt, cache_a_k, cache_a_v, cache_a_kidx, cache_b_k, cache_b_v,
                 cache_mem_k, cache_mem_v, rel_table, ln_g, w_in, w_out, a_qn, a_kn, b_qn, b_kn, b_subln,
                 lam_q1, lam_k1, lam_q2, lam_k2, c_qn, c_kn, mem_ln, w_mem_kv):
    f = np.float32
    xp = np.asarray(x_prompt, f)[0]
    xs = np.asarray(x_sample, f)
    xpT = np.ascontiguousarray(xp.T)
    cstv, oneh = _host_consts()
    gains = np.concatenate([np.tile(np.asarray(a_qn, f)[0], 8), np.tile(np.asarray(a_kn, f)[0], 8),
                            np.tile(np.asarray(b_qn, f)[0], 8), np.tile(np.asarray(b_kn, f)[0], 8),
                            np.tile(np.asarray(c_qn, f)[0], 4), np.tile(np.asarray(c_kn, f)[0], 4),
                            np.tile(np.asarray(b_subln, f)[0], 4)])
    gains = np.ascontiguousarray(np.broadcast_to(gains[None, :], (128, NGAIN)))
    lamv = np.concatenate([np.asarray(v, f)[0] for v in (lam_q1, lam_k1, lam_q2, lam_k2)])
    lamv = np.ascontiguousarray(np.broadcast_to(lamv[None, :], (128, 128)))
    tabp = np.zeros((128, 128), f)
    tabp[0:32, 0:12] = np.asarray(rel_table, f)
    lng = np.concatenate([np.asarray(ln_g, f)[0].reshape(8, 128).T, np.asarray(mem_ln, f)[0].reshape(8, 128).T], axis=1)
    lng = np.ascontiguousarray(lng)
    memT = np.ascontiguousarray(np.asarray(mem_prompt, f)[0].T)
    shared = {"memT": memT, "w_in": np.ascontiguousarray(np.asarray(w_in, f)[0]),
              "w_out": np.ascontiguousarray(np.asarray(w_out, f)[0]),
              "w_mem": np.ascontiguousarray(np.asarray(w_mem_kv, f)[0]), "lng": lng, "gains": gains, "lamv": lamv,
              "tabp": tabp, "oneh": oneh, "cst": cstv}
    cak = np.asarray(cache_a_k, f)[0]
    cav = np.asarray(cache_a_v, f)[0]
    cik = np.asarray(cache_a_kidx, f)[0]
    cbk = np.asarray(cache_b_k, f)[0]
    cbv = np.asarray(cache_b_v, f)[0]
    cmk = np.asarray(cache_mem_k, f)[0]
    cmv = np.asarray(cache_mem_v, f)[0]
    in_maps = []
    for c in range(NCORE):
        m = dict(shared)
        sh = 7 - c
        xT_all = np.zeros((D, P_ROWS), f)
        nreal = P_ROWS - sh * 128
        xT_all[:, sh * 128:] = xpT[:, :nreal]
        m["xT_all"] = xT_all
        valid = np.ones((128, 128), f)
        valid[:, :sh] = 0.0
        m["valid"] = valid
        xq = np.zeros((NJOB * 128, D), f)
        for s in range(NSLOT):
            blk = 8 * s + c
            xq[s * 128:(s + 1) * 128] = xp[blk * 128:(blk + 1) * 128]
        for b in range(NSAMP):
            xq[(NSLOT + b) * 128 + 64:(NSLOT + b + 1) * 128] = xs[4 * c + b]
        m["xq"] = xq
        m["xqT"] = np.ascontiguousarray(xq.T)
        fm = np.zeros((128, 2, 896), f)
        fm[:, 0, :sh * 128] = -1e30
        fm[:, 1, :448] = -1e30
        m["fmask"] = fm.reshape(128, 1792)
        ckT = np.zeros((NSAMP, 128, 7, S_ROWS), f)
        cv = np.zeros((NSAMP, S_ROWS, 12, 65), f)
        cmkT = np.zeros((NSAMP, 128, 2, 256), f)
        cmvv = np.zeros((NSAMP, 256, 4, 65), f)
        for b in range(NSAMP):
            sq = 4 * c + b
            ckT[b, :, 0:4, 448:1472] = cak[sq].reshape(1024, 4, 128).transpose(2, 1, 0)
            ckT[b, :, 4:6, 448:1472] = cbk[sq].reshape(1024, 2, 128).transpose(2, 1, 0)
            ckT[b, :, 6, 448:1472] = np.tile(cik[sq].T, (4, 1))
            cv[b, 448:1472, 0:8, 0:64] = cav[sq]
            cv[b, 448:1472, 8:12, 0:64] = cbv[sq]
            cv[b, 448:1472, :, 64] = 1.0
            cmkT[b] = cmk[sq].reshape(256, 2, 128).transpose(2, 1, 0)
            cmvv[b, :, :, 0:64] = cmv[sq]
            cmvv[b, :, :, 64] = 1.0
        m["c_kT"] = ckT
        m["c_v"] = cv.reshape(NSAMP, S_ROWS, 780)
        m["c_mkT"] = cmkT
        m["c_mv"] = cmvv.reshape(NSAMP, 256, 260)
        in_maps.append(m)
    return in_maps


def _assemble(results):
    f = np.float32
    y_p = np.zeros((1, SEQ, D), f)
    y_s = np.zeros((32, 64, D), f)
    pak = np.zeros((SEQ, 512), f)
    pav = np.zeros((SEQ, 512), f)
    pik = np.zeros((SEQ, 32), f)
    pbk = np.zeros((SEQ, 256), f)
    pbv = np.zeros((SEQ, 256), f)
    sak = np.zeros((32, 64, 512), f)
    sav = np.zeros((32, 64, 512), f)
    sik = np.zeros((32, 64, 32), f)
    sbk = np.zeros((32, 64, 256), f)
    sbv = np.zeros((32, 64, 256), f)
    for c in range(NCORE):
        r = results[c]
        for s in range(NSLOT):
            blk = 8 * s + c
            sl = slice(blk * 128, (blk + 1) * 128)
            js = slice(s * 128, (s + 1) * 128)
            y_p[0, sl] = r["y"][js]
            pak[sl] = r["o_ak"][js]
            pav[sl] = r["o_av"][js]
            pik[sl] = r["o_ik"][js]
            pbk[sl] = r["o_bk"][js]
            pbv[sl] = r["o_bv"][js]
        for b in range(NSAMP):
            js = slice((NSLOT + b) * 128 + 64, (NSLOT + b + 1) * 128)
            sq = 4 * c + b
            y_s[sq] = r["y"][js]
            sak[sq] = r["o_ak"][js]
            sav[sq] = r["o_av"][js]
            sik[sq] = r["o_ik"][js]
            sbk[sq] = r["o_bk"][js]
            sbv[sq] = r["o_bv"][js]
    r0 = results[0]
    return (y_p, y_s,
            pak.reshape(1, 1, SEQ, 8, 64), pav.reshape(1, 1, SEQ, 8, 64), pik.reshape(1, 1, SEQ, 32),
            pbk.reshape(1, 1, SEQ, 4, 2, 32), pbv.reshape(1, 1, SEQ, 4, 64),
            np.asarray(r0["o_mk"], f).reshape(1, 1, 256, 4, 64), np.asarray(r0["o_mv"], f).reshape(1, 1, 256, 4, 64),
            sak.reshape(1, 32, 64, 8, 64), sav.reshape(1, 32, 64, 8, 64), sik.reshape(1, 32, 64, 32),
            sbk.reshape(1, 32, 64, 4, 2, 32), sbv.reshape(1, 32, 64, 4, 64))


def kernel(**inputs):
    in_maps = _prep_inputs(**inputs)
    nc = build_nc()
    res = run_bass_kernel_spmd(nc, in_maps, core_ids=list(range(NCORE)))
    return _assemble(res.results)
```
